# Optimizing a Trainium2 kernel written in Bass

```python
import math
import jax, jax.numpy as jnp
from jax import lax
import numpy as np

D_MODEL = 2048
BATCH = 16
SEQ = 256
DEPTH = 2
DEC_BATCH = 4
DEC_SEQ = 1024
PAST_LEN = 256

GRID_W = 64
HEAD_DIM = 128
BLOCK = 128
ROPE_THETA = 10000.0
EPS = 1e-6
A_HEADS = 8
A_KV = 2
A_WINDOW = 128
C_HEADS = 8
C_KV = 2
SSM_HEADS = 32
SSM_HEAD_DIM = 64
SSM_GROUPS = 2
SSM_STATE = 128
SSM_CHUNK = 128
CONV_K = 3
D_FF = 4 * D_MODEL

A_QW = A_HEADS * HEAD_DIM
A_KVW = A_KV * HEAD_DIM
C_QW = C_HEADS * HEAD_DIM
C_KVW = C_KV * HEAD_DIM
SSM_INNER = SSM_HEADS * SSM_HEAD_DIM
CONV_CH = SSM_INNER + 2 * SSM_GROUPS * SSM_STATE
N_IN = A_QW + 2 * A_KVW + SSM_INNER + CONV_CH + 2 * SSM_HEADS + C_QW + 2 * C_KVW + 3 * D_MODEL

kernel_name = 'hybrid_diffusion_prefix_trunk_step'


def rms_norm(x, g):
    xf = x.astype(jnp.float32)
    y = xf * lax.rsqrt(jnp.mean(xf * xf, axis=-1, keepdims=True) + EPS)
    return (y * g.astype(jnp.float32)).astype(x.dtype)


def axial_rope_tables(n_tokens):
    rows = n_tokens // GRID_W
    row = jnp.repeat(jnp.arange(rows, dtype=jnp.float32), GRID_W)
    col = jnp.tile(jnp.arange(GRID_W, dtype=jnp.float32), rows)
    n_freq = HEAD_DIM // 4
    inv_freq = ROPE_THETA ** (-jnp.arange(n_freq, dtype=jnp.float32) / n_freq)
    ang = jnp.concatenate([row[:, None] * inv_freq, col[:, None] * inv_freq], axis=-1)
    return jnp.cos(ang), jnp.sin(ang)


def apply_rope(x, cos, sin):
    half = HEAD_DIM // 2
    xf = x.astype(jnp.float32)
    x1, x2 = xf[..., :half], xf[..., half:]
    c = cos[None, :, None, :]
    s = sin[None, :, None, :]
    return jnp.concatenate([x1 * c - x2 * s, x2 * c + x1 * s], axis=-1).astype(x.dtype)


def dense_attention(q, k, v, sink):
    b, lq, nh, hd = q.shape
    nkv = k.shape[2]
    g = nh // nkv
    nb = lq // BLOCK
    scale = hd ** -0.5
    qb = q.reshape(b, nb, BLOCK, nkv, g, hd).transpose(1, 0, 2, 3, 4, 5)

    def one_block(qi):
        s = jnp.einsum('bqhgd,bkhd->bhgqk', qi, k).astype(jnp.float32) * scale
        if sink is not None:
            sk = jnp.broadcast_to(sink.astype(jnp.float32).reshape(1, nkv, g, 1, 1), s.shape[:-1] + (1,))
            p = jax.nn.softmax(jnp.concatenate([s, sk], axis=-1), axis=-1)[..., :-1]
        else:
            p = jax.nn.softmax(s, axis=-1)
        return jnp.einsum('bhgqk,bkhd->bqhgd', p.astype(v.dtype), v)

    out = lax.map(one_block, qb)
    return out.transpose(1, 0, 2, 3, 4, 5).reshape(b, lq, nh, hd)


def banded_attention_with_context(q, k, v, k_ctx, v_ctx, sink):
    b, l, nh, hd = q.shape
    nkv = k.shape[2]
    g = nh // nkv
    nb = l // BLOCK
    scale = hd ** -0.5
    qb = q.reshape(b, nb, BLOCK, nkv, g, hd)
    pad = jnp.zeros((b, BLOCK, nkv, hd), k.dtype)

    def windows(t):
        tb = jnp.concatenate([pad, t, pad], axis=1).reshape(b, nb + 2, BLOCK, nkv, hd)
        return jnp.concatenate([tb[:, :-2], tb[:, 1:-1], tb[:, 2:]], axis=2)

    kw, vw = windows(k), windows(v)
    s_loc = jnp.einsum('bnqhgd,bnkhd->bnhgqk', qb, kw).astype(jnp.float32) * scale
    blk = jnp.arange(nb)[:, None, None]
    qpos = blk * BLOCK + jnp.arange(BLOCK)[None, :, None]
    kpos = (blk - 1) * BLOCK + jnp.arange(3 * BLOCK)[None, None, :]
    valid = (jnp.abs(kpos - qpos) <= A_WINDOW) & (kpos >= 0) & (kpos < l)
    s_loc = jnp.where(valid[None, :, None, None], s_loc, -jnp.inf)
    s_ctx = jnp.einsum('bnqhgd,bkhd->bnhgqk', qb, k_ctx).astype(jnp.float32) * scale
    sk = jnp.broadcast_to(sink.astype(jnp.float32).reshape(1, 1, nkv, g, 1, 1), s_loc.shape[:-1] + (1,))
    p = jax.nn.softmax(jnp.concatenate([s_ctx, s_loc, sk], axis=-1), axis=-1)
    n_ctx = k_ctx.shape[1]
    p_ctx = p[..., :n_ctx].astype(v.dtype)
    p_loc = p[..., n_ctx:n_ctx + 3 * BLOCK].astype(v.dtype)
    out = jnp.einsum('bnhgqk,bkhd->bnqhgd', p_ctx, v_ctx) + jnp.einsum('bnhgqk,bnkhd->bnqhgd', p_loc, vw)
    return out.reshape(b, l, nh, hd)


def depthwise_conv_silu(u, w, bias):
    out = lax.conv_general_dilated(u, w[:, None, :], window_strides=(1,), padding=[(CONV_K // 2, CONV_K // 2)],
                                   dimension_numbers=('NWC', 'WIO', 'NWC'), feature_group_count=u.shape[-1])
    return jax.nn.silu(out + bias)


def ssd_scan(x, dt, a_log, bm, cm, h0):
    b, l, nh, p = x.shape
    ng, n = bm.shape[2], bm.shape[3]
    hg = nh // ng
    q = SSM_CHUNK
    nc = l // q
    f32 = jnp.float32
    a = dt * (-jnp.exp(a_log.astype(f32)))
    a = a.reshape(b, nc, q, ng, hg).transpose(0, 3, 4, 1, 2)
    xdt = (x.astype(f32) * dt[..., None]).reshape(b, nc, q, ng, hg, p)
    bm = bm.astype(f32).reshape(b, nc, q, ng, n)
    cm = cm.astype(f32).reshape(b, nc, q, ng, n)
    a_cum = jnp.cumsum(a, axis=-1)
    lower = jnp.tril(jnp.ones((q, q), dtype=bool))
    seg = jnp.exp(jnp.where(lower, a_cum[..., :, None] - a_cum[..., None, :], -jnp.inf))
    cb = jnp.einsum('bclgn,bcsgn->bgcls', cm, bm)
    y_diag = jnp.einsum('bgjcls,bcsgjp->bclgjp', cb[:, :, None] * seg, xdt)
    to_end = jnp.exp(a_cum[..., -1:] - a_cum).transpose(0, 3, 4, 1, 2)
    chunk_states = jnp.einsum('bcsgn,bcsgjp->cbgjpn', bm, xdt * to_end[..., None])
    chunk_decay = jnp.exp(a_cum[..., -1]).transpose(3, 0, 1, 2)

    def step(h, inp):
        st, dc = inp
        return dc[..., None, None] * h + st, h

    h_fin, h_enter = lax.scan(step, h0.astype(f32).reshape(b, ng, hg, p, n), (chunk_states, chunk_decay))
    from_start = jnp.exp(a_cum).transpose(0, 3, 4, 1, 2)
    y_off = jnp.einsum('bclgn,cbgjpn->bclgjp', cm, h_enter) * from_start[..., None]
    return (y_diag + y_off).reshape(b, l, nh, p), h_fin.reshape(b, nh, p, n)


def ssd_bidirectional(xbc, dt_raw, z, h0, lp):
    b, l, _ = xbc.shape
    gn = SSM_GROUPS * SSM_STATE
    xs = xbc[..., :SSM_INNER].reshape(b, l, SSM_HEADS, SSM_HEAD_DIM)
    bm = xbc[..., SSM_INNER:SSM_INNER + gn].reshape(b, l, SSM_GROUPS, SSM_STATE)
    cm = xbc[..., SSM_INNER + gn:].reshape(b, l, SSM_GROUPS, SSM_STATE)
    dt = jax.nn.softplus(dt_raw.reshape(b, l, 2, SSM_HEADS).astype(jnp.float32) + lp['ssm_dt_bias'].astype(jnp.float32))
    y_f, h_f = ssd_scan(xs, dt[:, :, 0], lp['ssm_a_log'][0], bm, cm, h0[:, 0])
    y_b, h_b = ssd_scan(jnp.flip(xs, 1), jnp.flip(dt[:, :, 1], 1), lp['ssm_a_log'][1],
                        jnp.flip(bm, 1), jnp.flip(cm, 1), h0[:, 1])
    y = y_f + jnp.flip(y_b, 1) + lp['ssm_d'].astype(jnp.float32)[:, None] * xs.astype(jnp.float32)
    y = y.reshape(b, l, SSM_INNER) * jax.nn.silu(z.astype(jnp.float32))
    y = rms_norm(y, lp['ssm_norm_g']).astype(xbc.dtype)
    return y, jnp.stack([h_f, h_b], axis=1)


def split_projection(pr):
    sizes = [A_QW, A_KVW, A_KVW, SSM_INNER, CONV_CH, 2 * SSM_HEADS, C_QW, C_KVW, C_KVW, D_MODEL, D_MODEL, D_MODEL]
    idx = np.cumsum(sizes)[:-1].tolist()
    return jnp.split(pr, idx, axis=-1)


def modulation(cond, lp):
    mod = jax.nn.silu(cond) @ lp['w_ada'] + lp['b_ada']
    return jnp.split(mod, 6, axis=-1)


def mixer_inputs(x, sh1, sc1, lp):
    h = rms_norm(x, lp['norm1_g']) * (1 + sc1) + sh1
    return split_projection(h @ lp['w_in'])


def merge_and_ffn(x, ya, yb, yc, ga, gb, gc, g1, sh2, sc2, g2, lp):
    b, l, _ = x.shape
    br_a = ya.reshape(b, l, A_QW) @ lp['w_oa']
    br_b = yb @ lp['w_ob']
    br_c = yc.reshape(b, l, C_QW) @ lp['w_oc']
    merged = jax.nn.sigmoid(ga) * br_a + jax.nn.sigmoid(gb) * br_b + jax.nn.sigmoid(gc) * br_c
    x = x + g1 * (merged @ lp['w_out'])
    h2 = rms_norm(x, lp['norm2_g']) * (1 + sc2) + sh2
    f = jnp.square(jax.nn.relu(h2 @ lp['w_mlp1'])) @ lp['w_mlp2']
    return x + g2 * f


def context_layer(x, c_ctx, lp):
    b, l, _ = x.shape
    sh1, sc1, g1, sh2, sc2, g2 = modulation(c_ctx, lp)
    qa, ka, va, z, xbc, dt_raw, qc, kc, vc, ga, gb, gc = mixer_inputs(x, sh1, sc1, lp)
    ka = ka.reshape(b, l, A_KV, HEAD_DIM)
    va = va.reshape(b, l, A_KV, HEAD_DIM)
    ya = dense_attention(qa.reshape(b, l, A_HEADS, HEAD_DIM), ka, va, lp['a_sink'])
    xbc = depthwise_conv_silu(xbc, lp['conv_w'], lp['conv_b'])
    h0 = jnp.zeros((b, 2, SSM_HEADS, SSM_HEAD_DIM, SSM_STATE), jnp.float32)
    yb, ssm_state = ssd_bidirectional(xbc, dt_raw, z, h0, lp)
    qc = rms_norm(qc.reshape(b, l, C_HEADS, HEAD_DIM), lp['c_q_norm'])
    kc = rms_norm(kc.reshape(b, l, C_KV, HEAD_DIM), lp['c_k_norm'])
    vc = vc.reshape(b, l, C_KV, HEAD_DIM)
    yc = dense_attention(qc, kc, vc, None)
    x = merge_and_ffn(x, ya, yb, yc, ga, gb, gc, g1, sh2, sc2, g2, lp)
    return x, (ka, va, kc, vc, ssm_state.astype(x.dtype))


def latent_layer(x, c, ak_ctx, av_ctx, ck_ctx, cv_ctx, h0, cos, sin, lp):
    b, l, _ = x.shape
    sh1, sc1, g1, sh2, sc2, g2 = [m[:, None, :] for m in modulation(c, lp)]
    qa, ka, va, z, xbc, dt_raw, qc, kc, vc, ga, gb, gc = mixer_inputs(x, sh1, sc1, lp)
    qa = apply_rope(qa.reshape(b, l, A_HEADS, HEAD_DIM), cos, sin)
    ka = apply_rope(ka.reshape(b, l, A_KV, HEAD_DIM), cos, sin)
    va = va.reshape(b, l, A_KV, HEAD_DIM)
    ya = banded_attention_with_context(qa, ka, va, ak_ctx, av_ctx, lp['a_sink'])
    xbc = depthwise_conv_silu(xbc, lp['conv_w'], lp['conv_b'])
    yb, _ = ssd_bidirectional(xbc, dt_raw, z, h0, lp)
    qc = apply_rope(rms_norm(qc.reshape(b, l, C_HEADS, HEAD_DIM), lp['c_q_norm']), cos, sin)
    kc = apply_rope(rms_norm(kc.reshape(b, l, C_KV, HEAD_DIM), lp['c_k_norm']), cos, sin)
    vc = vc.reshape(b, l, C_KV, HEAD_DIM)
    yc = dense_attention(qc, jnp.concatenate([ck_ctx, kc], axis=1), jnp.concatenate([cv_ctx, vc], axis=1), None)
    return merge_and_ffn(x, ya, yb, yc, ga, gb, gc, g1, sh2, sc2, g2, lp)


def setup_inputs(seed: int = 0) -> dict:
    key = jax.random.key(seed)
    ks = jax.random.split(key, 32)
    f32 = jnp.float32

    def nrm(k, shape, scale):
        return scale * jax.random.normal(k, shape, f32)

    dt0 = jnp.exp(jax.random.uniform(ks[16], (DEPTH, 2, SSM_HEADS), f32, math.log(1e-3), math.log(1e-1)))
    return {
        'x_prompt': nrm(ks[0], (BATCH, SEQ, D_MODEL), 1.0),
        'x_sample': nrm(ks[1], (DEC_BATCH, DEC_SEQ, D_MODEL), 1.0),
        'cache_a_k': nrm(ks[2], (DEC_BATCH, DEPTH, PAST_LEN, A_KV, HEAD_DIM), 1.0),
        'cache_a_v': nrm(ks[3], (DEC_BATCH, DEPTH, PAST_LEN, A_KV, HEAD_DIM), 1.0),
        'cache_c_k': nrm(ks[4], (DEC_BATCH, DEPTH, PAST_LEN, C_KV, HEAD_DIM), 1.0),
        'cache_c_v': nrm(ks[5], (DEC_BATCH, DEPTH, PAST_LEN, C_KV, HEAD_DIM), 1.0),
        'state_ssm': nrm(ks[6], (DEC_BATCH, DEPTH, 2, SSM_HEADS, SSM_HEAD_DIM, SSM_STATE), 0.1),
        'c': nrm(ks[7], (DEC_BATCH, D_MODEL), 1.0),
        'c_ctx': nrm(ks[8], (D_MODEL,), 1.0),
        'norm1_g': 1.0 + nrm(ks[9], (DEPTH, D_MODEL), 0.02),
        'w_ada': nrm(ks[10], (DEPTH, D_MODEL, 6 * D_MODEL), 0.5 * D_MODEL ** -0.5),
        'b_ada': nrm(ks[11], (DEPTH, 6 * D_MODEL), 0.02),
        'w_in': nrm(ks[12], (DEPTH, D_MODEL, N_IN), D_MODEL ** -0.5),
        'a_sink': nrm(ks[13], (DEPTH, A_HEADS), 0.5),
        'conv_w': nrm(ks[14], (DEPTH, CONV_K, CONV_CH), CONV_K ** -0.5),
        'conv_b': nrm(ks[15], (DEPTH, CONV_CH), 0.01),
        'ssm_a_log': jnp.log(jax.random.uniform(ks[17], (DEPTH, 2, SSM_HEADS), f32, 1.0, 16.0)),
        'ssm_dt_bias': dt0 + jnp.log(-jnp.expm1(-dt0)),
        'ssm_d': 1.0 + nrm(ks[18], (DEPTH, SSM_HEADS), 0.1),
        'ssm_norm_g': 1.0 + nrm(ks[19], (DEPTH, SSM_INNER), 0.02),
        'c_q_norm': 1.0 + nrm(ks[20], (DEPTH, HEAD_DIM), 0.02),
        'c_k_norm': 1.0 + nrm(ks[21], (DEPTH, HEAD_DIM), 0.02),
        'w_oa': nrm(ks[22], (DEPTH, A_QW, D_MODEL), A_QW ** -0.5),
        'w_ob': nrm(ks[23], (DEPTH, SSM_INNER, D_MODEL), SSM_INNER ** -0.5),
        'w_oc': nrm(ks[24], (DEPTH, C_QW, D_MODEL), C_QW ** -0.5),
        'w_out': nrm(ks[25], (DEPTH, D_MODEL, D_MODEL), D_MODEL ** -0.5),
        'norm2_g': 1.0 + nrm(ks[26], (DEPTH, D_MODEL), 0.02),
        'w_mlp1': nrm(ks[27], (DEPTH, D_MODEL, D_FF), D_MODEL ** -0.5),
        'w_mlp2': nrm(ks[28], (DEPTH, D_FF, D_MODEL), D_FF ** -0.5),
        'final_norm_g': 1.0 + nrm(ks[29], (D_MODEL,), 0.02),
    }


def reference(x_prompt, x_sample, cache_a_k, cache_a_v, cache_c_k, cache_c_v, state_ssm, c, c_ctx,
              norm1_g, w_ada, b_ada, w_in, a_sink, conv_w, conv_b, ssm_a_log, ssm_dt_bias, ssm_d, ssm_norm_g,
              c_q_norm, c_k_norm, w_oa, w_ob, w_oc, w_out, norm2_g, w_mlp1, w_mlp2, final_norm_g):
    cos, sin = axial_rope_tables(x_sample.shape[1])
    yp, ys = x_prompt, x_sample
    new_ak, new_av, new_ck, new_cv, new_st = [], [], [], [], []
    for layer in range(DEPTH):
        lp = {
            'norm1_g': norm1_g[layer], 'w_ada': w_ada[layer], 'b_ada': b_ada[layer], 'w_in': w_in[layer],
            'a_sink': a_sink[layer], 'conv_w': conv_w[layer], 'conv_b': conv_b[layer],
            'ssm_a_log': ssm_a_log[layer], 'ssm_dt_bias': ssm_dt_bias[layer], 'ssm_d': ssm_d[layer],
            'ssm_norm_g': ssm_norm_g[layer], 'c_q_norm': c_q_norm[layer], 'c_k_norm': c_k_norm[layer],
            'w_oa': w_oa[layer], 'w_ob': w_ob[layer], 'w_oc': w_oc[layer], 'w_out': w_out[layer],
            'norm2_g': norm2_g[layer], 'w_mlp1': w_mlp1[layer], 'w_mlp2': w_mlp2[layer],
        }
        yp, (ka, va, kc, vc, st) = context_layer(yp, c_ctx, lp)
        new_ak.append(ka)
        new_av.append(va)
        new_ck.append(kc)
        new_cv.append(vc)
        new_st.append(st)
        ys = latent_layer(ys, c, cache_a_k[:, layer], cache_a_v[:, layer], cache_c_k[:, layer], cache_c_v[:, layer],
                          state_ssm[:, layer], cos, sin, lp)
    y_prompt = rms_norm(yp, final_norm_g)
    y_sample = rms_norm(ys, final_norm_g)
    return (y_prompt, y_sample, jnp.stack(new_ak, axis=1), jnp.stack(new_av, axis=1), jnp.stack(new_ck, axis=1),
            jnp.stack(new_cv, axis=1), jnp.stack(new_st, axis=1))
```

```python
import math
import numpy as np
import concourse.bass as bass
import concourse.mybir as mybir
from concourse.bass_utils import run_bass_kernel_spmd

F32 = mybir.dt.float32
BF16 = mybir.dt.bfloat16
AF = mybir.ActivationFunctionType
ALU = mybir.AluOpType

D = 2048
T = 1024
NKC = 16
DEPTH = 2
EPS = 1e-6
NEG = -30000.0
O_QA, O_KA, O_VA, O_Z, O_XBC, O_DT, O_QC, O_KC, O_VC, O_GA, O_GB, O_GC = (
    0, 1024, 1280, 1536, 3584, 6144, 6208, 7232, 7488, 7744, 9792, 11840)
O_B = O_XBC + 2048
O_C = O_XBC + 2048 + 256
N_IN = 13888


class V:
    def __init__(self, t, off, pstride, p0, npart, dims, aid=None):
        self.t, self.off, self.pstride, self.p0, self.np, self.dims = t, off, pstride, p0, npart, list(dims)
        self.aid = aid

    def __getitem__(self, idx):
        if not isinstance(idx, tuple):
            idx = (idx,)
        idx = list(idx) + [slice(None)] * (1 + len(self.dims) - len(idx))
        ps = idx[0]
        p0, npart = self.p0, self.np
        if isinstance(ps, slice):
            a = 0 if ps.start is None else ps.start
            b = self.np if ps.stop is None else ps.stop
            p0, npart = self.p0 + a, b - a
        else:
            p0, npart = self.p0 + ps, 1
        off = self.off
        dims = []
        for (st, n), ix in zip(self.dims, idx[1:]):
            if isinstance(ix, slice):
                a = 0 if ix.start is None else ix.start
                b = n if ix.stop is None else ix.stop
                step = 1 if ix.step is None else ix.step
                cnt = (b - a + step - 1) // step
                off += a * st
                dims.append((st * step, cnt))
            else:
                off += ix * st
        return V(self.t, off, self.pstride, p0, npart, dims, self.aid)

    def bc(self, pos, n):
        d = list(self.dims)
        d.insert(pos, (0, n))
        return V(self.t, self.off, self.pstride, self.p0, self.np, d, self.aid)

    def split(self, pos, inner):
        st, n = self.dims[pos]
        d = list(self.dims)
        d[pos:pos + 1] = [(st * inner, n // inner), (st, inner)]
        return V(self.t, self.off, self.pstride, self.p0, self.np, d, self.aid)

    @property
    def ap(self):
        dims = [[st, n] for st, n in self.dims]
        out = []
        for st, n in dims:
            if out and out[-1][0] == st * n and st != 0:
                out[-1] = [st, out[-1][1] * n]
            else:
                out.append([st, n])
        if not out:
            out = [[1, 1]]
        return bass.AP(self.t, self.off + self.p0 * self.pstride, [[self.pstride, self.np]] + out)


class Arena:
    def __init__(self, nc, name, nbytes):
        self.t16 = nc.alloc_sbuf_tensor(name, [128, nbytes // 2], BF16)
        self.t32 = self.t16.bitcast(F32)
        self.nbytes = nbytes
        self.top = 0
        self.peak = 0
        self.live = []
        self.retired = []
        self.prior = {}
        self.naid = 0

    def alloc(self, dims, dtype, npart=128):
        n = int(np.prod(dims))
        esz = 4 if dtype == F32 else 2
        self.top = (self.top + 31) // 32 * 32
        boff = self.top
        self.top += n * esz
        self.peak = max(self.peak, self.top)
        assert self.top <= self.nbytes, f"arena overflow {self.top} > {self.nbytes}"
        strides = []
        s = 1
        for d_ in reversed(dims):
            strides.append(s)
            s *= d_
        strides = strides[::-1]
        t = self.t32 if dtype == F32 else self.t16
        self.naid += 1
        aid = self.naid
        end = boff + n * esz
        pr = [a for (a, s0, e0) in self.retired if s0 < end and boff < e0]
        if pr:
            self.prior[aid] = pr
        self.live.append((aid, boff, end))
        return V(t, boff // esz, self.nbytes // esz, 0, npart, list(zip(strides, dims)), aid)

    def mark(self):
        return self.top

    def release(self, m):
        self.top = m
        keep = []
        for ent in self.live:
            if ent[1] >= m:
                self.retired.append(ent)
            else:
                keep.append(ent)
        self.live = keep


class Op:
    __slots__ = ("eng", "fn", "deps", "signal", "count", "sem", "semval", "dma", "tag", "nins")


class Prog:
    ENGS = ("pe", "act", "dve", "pool", "sp")

    def __init__(self):
        self.ops = []
        self.lastw = {}
        self.readers = {}
        self.ndma = {"pool": 0, "sp": 0, "act": 0}
        self.users = {}
        self.arena = None

    def add(self, eng, fn, reads=(), writes=(), dma=False, aids_r=(), aids_w=()):
        op = Op()
        op.eng, op.fn, op.signal, op.count, op.dma = eng, fn, False, 0, dma
        op.sem, op.semval = None, 0
        op.tag, op.nins = getattr(self, "tag", ""), 1
        deps = []
        for r in reads:
            w = self.lastw.get(r)
            if w is not None:
                deps.append(w)
        for w_ in writes:
            w = self.lastw.get(w_)
            if w is not None:
                deps.append(w)
            deps.extend(self.readers.get(w_, ()))
        for r in reads:
            self.readers.setdefault(r, []).append(op)
        for w_ in writes:
            self.lastw[w_] = op
            self.readers[w_] = []
        for a in list(aids_w) + list(aids_r):
            if a is None:
                continue
            for p in self.arena.prior.get(a, ()):
                u = self.users.get(p)
                if u:
                    deps.extend(u["eng"].values())
                    deps.extend(u["dma"])
        for a in list(aids_w) + list(aids_r):
            if a is None:
                continue
            u = self.users.setdefault(a, {"eng": {}, "dma": []})
            if dma:
                u["dma"].append(op)
            else:
                u["eng"][eng] = op
        seen = set()
        op.deps = []
        for d_ in deps:
            if d_ is op or id(d_) in seen:
                continue
            seen.add(id(d_))
            if d_.eng == "pe" and eng == "pe" and not d_.dma:
                continue
            op.deps.append(d_)
            d_.signal = True
        self.ops.append(op)
        return op

    def emit(self, nc, final_wait_eng="sp"):
        NDS = 16
        engsem = {e: nc.alloc_semaphore(f"s_{e}") for e in ("pe", "act", "dve", "pool")}
        dmasem = {q: [nc.alloc_semaphore(f"d_{q}{i}") for i in range(NDS)] for q in ("pool", "sp", "act")}
        dmacnt = {q: [0] * NDS for q in dmasem}
        cnt = {e: 0 for e in engsem}
        rr = {q: 0 for q in dmasem}
        for op in self.ops:
            if op.dma:
                q = op.eng
                k = rr[q] % NDS
                rr[q] += 1
                dmacnt[q][k] += 1
                op.sem, op.semval = dmasem[q][k], 16 * dmacnt[q][k]
            else:
                if op.signal:
                    cnt[op.eng] += 1
                    op.sem, op.semval = engsem[op.eng], cnt[op.eng]
        byeng = {e: [o for o in self.ops if o.eng == e] for e in self.ENGS}
        handles = {"pe": "tensor", "act": "scalar", "dve": "vector", "pool": "gpsimd", "sp": "sync"}
        finals = []
        for q in dmasem:
            for k in range(NDS):
                if dmacnt[q][k]:
                    finals.append((dmasem[q][k], 16 * dmacnt[q][k]))

        def run(engname, e):
            waited = {}
            for op in byeng[engname]:
                for d_ in op.deps:
                    key = d_.sem.num
                    if waited.get(key, 0) < d_.semval:
                        e.wait_ge(d_.sem, d_.semval)
                        waited[key] = d_.semval
                if op.dma and op.semval > 16:
                    key = op.sem.num
                    if waited.get(key, 0) < op.semval - 16:
                        e.wait_ge(op.sem, op.semval - 16)
                        waited[key] = op.semval - 16
                inst = op.fn(e)
                if op.dma:
                    inst.then_inc(op.sem, 16)
                elif op.signal:
                    inst.then_inc(op.sem, 1)
            if engname == final_wait_eng:
                for sem, val in finals:
                    e.wait_ge(sem, val)
                for en, c in cnt.items():
                    if c:
                        e.wait_ge(engsem[en], c)

        with nc.Block() as block:
            @block.tensor
            def _(e):
                run("pe", e)

            @block.scalar
            def _(e):
                run("act", e)

            @block.vector
            def _(e):
                run("dve", e)

            @block.gpsimd
            def _(e):
                run("pool", e)

            @block.sync
            def _(e):
                run("sp", e)


class K:
    def __init__(self, dbg=None, nlayers=DEPTH, phases=None, skip=()):
        self.dbg = dbg or {}
        self.skip = set(skip)
        self.nlayers = nlayers
        self.phases = phases
        self.nc = bass.Bass("TRN2", target_bir_lowering=False)
        self.P = Prog()
        self.inputs = {}
        self.outputs = {}
        self.uid = 0

    def din(self, name, shape):
        if name in self.skip:
            shape = [1, 1]
        t = self.nc.dram_tensor(name, list(shape), F32, kind="ExternalInput")
        self.inputs[name] = tuple(shape)
        return t.ap()

    def dout(self, name, shape):
        t = self.nc.dram_tensor(name, list(shape), F32, kind="ExternalOutput")
        self.outputs[name] = tuple(shape)
        return t.ap()

    def dscratch(self, name, shape, dtype):
        return self.nc.dram_tensor(name, list(shape), dtype, kind="Internal").ap()

    def key(self, base):
        self.uid += 1
        return (base, self.uid)

    def _add(self, eng, fn, outs, ins, reads, writes, dma=False):
        aw = [v.aid for v in outs if isinstance(v, V)]
        ar = [v.aid for v in ins if isinstance(v, V)]
        reads = list(reads) + [("b", a) for a in ar if a is not None]
        writes = list(writes) + [("b", a) for a in aw if a is not None]
        return self.P.add(eng, fn, reads, writes, dma=dma, aids_r=ar, aids_w=aw)

    def act(self, out, in_, func, reads=(), writes=(), bias=None, scale=None):
        kw = {}
        if bias is not None:
            kw["bias"] = bias.ap if isinstance(bias, V) else bias
        if scale is not None:
            kw["scale"] = scale.ap if isinstance(scale, V) else scale
        o, i = out.ap, in_.ap
        self._add("act", lambda e: e.activation(out=o, in_=i, func=func, **kw), [out], [in_, bias, scale], reads, writes)

    def tt(self, out, a, b, op, reads=(), writes=(), eng="dve"):
        o, x, y = out.ap, a.ap, b.ap
        self._add(eng, lambda e: e.tensor_tensor(out=o, in0=x, in1=y, op=op), [out], [a, b], reads, writes)

    def ts(self, out, a, s1, op0, reads=(), writes=(), s2=None, op1=None, eng="dve"):
        o, x = out.ap, a.ap
        s1a = s1.ap if isinstance(s1, V) else s1
        s2a = s2.ap if isinstance(s2, V) else s2
        if op1 is None:
            fn = lambda e: e.tensor_scalar(out=o, in0=x, scalar1=s1a, scalar2=None, op0=op0)
        else:
            fn = lambda e: e.tensor_scalar(out=o, in0=x, scalar1=s1a, scalar2=s2a, op0=op0, op1=op1)
        self._add(eng, fn, [out], [a, s1, s2], reads, writes)

    def stt(self, out, a, s, b, op0, op1, reads=(), writes=()):
        o, x, y = out.ap, a.ap, b.ap
        sa = s.ap if isinstance(s, V) else s
        self._add("dve", lambda e: e.scalar_tensor_tensor(out=o, in0=x, scalar=sa, in1=y, op0=op0, op1=op1),
                  [out], [a, s, b], reads, writes)

    def copy(self, out, in_, reads=(), writes=(), eng="dve"):
        o, i = out.ap, in_.ap
        if eng == "act":
            self._add("act", lambda e: e.activation(out=o, in_=i, func=AF.Copy), [out], [in_], reads, writes)
        else:
            self._add(eng, lambda e: e.tensor_copy(out=o, in_=i), [out], [in_], reads, writes)

    def memset(self, out, val, writes=(), eng="dve"):
        o = out.ap
        self._add(eng, lambda e: e.memset(o, val), [out], [], (), writes)

    def scan(self, out, d0, d1, init, op0, op1, reads=(), writes=()):
        o, a, b = out.ap, d0.ap, d1.ap
        self._add("dve", lambda e: e.tensor_tensor_scan(out=o, data0=a, data1=b, initial=init, op0=op0, op1=op1),
                  [out], [d0, d1], reads, writes)

    def mm(self, out, pairs, reads=(), writes=()):
        o = out.ap
        pr = [(l.ap, r.ap) for l, r in pairs]

        def fn(e):
            inst = None
            for i, (l, r) in enumerate(pr):
                inst = e.matmul(o, l, r, start=(i == 0), stop=(i == len(pr) - 1))
            return inst
        self._add("pe", fn, [out], [v for p in pairs for v in p], reads, writes).nins = len(pr)

    def mm1(self, out, lhsT, rhs, start, stop, reads=(), writes=()):
        o, l, r = out.ap, lhsT.ap, rhs.ap
        self._add("pe", lambda e: e.matmul(o, l, r, start=start, stop=stop), [out], [lhsT, rhs], reads, writes)

    def transpose(self, out, in_, ident, reads=(), writes=()):
        o, i, d_ = out.ap, in_.ap, ident.ap
        self._add("pe", lambda e: e.transpose(o, i, d_), [out], [in_, ident], reads, writes)

    def dma(self, out, in_, reads=(), writes=(), q="sp"):
        o = out.ap if isinstance(out, V) else out
        i = in_.ap if isinstance(in_, V) else in_
        self._add(q, lambda e: e.dma_start(out=o, in_=i), [out], [in_], reads, writes, dma=True)

    def build(self):
        nc = self.nc
        NL = self.nlayers
        xT_d = self.din("xT", [D, T])
        cond_d = self.din("cond", [128, 16])
        cos_d = self.din("cosT", [128, T])
        sin_d = self.din("sinT", [128, T])
        akT_ctx_d = self.din("akT_ctx", [DEPTH, 2, 128, 256])
        av_ctx_d = self.din("av_ctx", [DEPTH, 2, 256, 128])
        ckT_ctx_d = self.din("ckT_ctx", [DEPTH, 2, 128, 256])
        cv_ctx_d = self.din("cv_ctx", [DEPTH, 2, 256, 128])
        h0T_d = self.din("h0T", [DEPTH, 2, 128, 2048])
        amprev_d = self.din("amprev", [128, 8, 128])
        amnext_d = self.din("amnext", [128, 8, 128])
        actxb_d = self.din("actxb", [128, 1])
        cbias_d = self.din("cbias", [128, 80])
        kmul_d = self.din("kmul", [128, 8, 2])
        negflag_d = self.din("negflag", [128, 1])
        n1g_d = self.din("n1g", [DEPTH, 128, 16])
        n2g_d = self.din("n2g", [DEPTH, 128, 16])
        fng_d = self.din("fng", [128, 16])
        badaT_d = self.din("badaT", [DEPTH, 128, 96])
        sink_d = self.din("sinkbc", [DEPTH, 128, 8])
        convw_d = self.din("convw", [DEPTH, 128, 20, 3])
        convb_d = self.din("convb", [DEPTH, 128, 20])
        alog_d = self.din("alog", [DEPTH, 64, 1])
        dtb_d = self.din("dtb", [DEPTH, 64, 1])
        dcol_d = self.din("dcol", [DEPTH, 128, 16])
        sng_d = self.din("sng", [DEPTH, 128, 16])
        cqn_d = self.din("cqn", [DEPTH, 128, 1])
        ckn_d = self.din("ckn", [DEPTH, 128, 1])
        w_ada_d = self.din("w_ada", [DEPTH, D, 6 * D])
        w_in_d = self.din("w_in", [DEPTH, D, N_IN])
        w_oa_d = self.din("w_oa", [DEPTH, 1024, D])
        w_ob_d = self.din("w_ob", [DEPTH, D, D])
        w_oc_d = self.din("w_oc", [DEPTH, 1024, D])
        w_out_d = self.din("w_out", [DEPTH, D, D])
        w_m1_d = self.din("w_mlp1", [DEPTH, D, 4 * D])
        w_m2_d = self.din("w_mlp2", [DEPTH, 4 * D, D])
        ident_d = self.din("ident", [128, 128])
        rot_d = self.din("rot", [128, 128])
        U_d = self.din("U", [128, 128])
        L_d = self.din("L", [128, 128])

        yT_o = self.dout("yT", [D, T])
        ak_o = self.dout("akT_o", [DEPTH, 2, 128, T])
        av_o = self.dout("avT_o", [DEPTH, 2, 128, T])
        ck_o = self.dout("ckT_o", [DEPTH, 2, 128, T])
        cv_o = self.dout("cvT_o", [DEPTH, 2, 128, T])
        st_o = self.dout("st_o", [DEPTH, 4, 2, 2048, 128])
        dbg_o = {n: self.dout("dbg_" + n, shp) for n, shp in self.dbg.items()}

        mrg_s = self.dscratch("mrg_s", [3, D, T], BF16)
        ygs_s = self.dscratch("ygs_s", [D, T], BF16)

        A = Arena(nc, "arena", 212480)
        self.A = A
        self.P.arena = A
        xT = A.alloc([16, T], F32)
        hT = A.alloc([16, T], BF16)
        WS = 4096
        NSLOT = 2
        wslots = [A.alloc([WS], BF16) for _ in range(NSLOT)]
        identb = A.alloc([128], BF16)
        onesb = A.alloc([128], BF16)
        rotb = A.alloc([128], BF16)
        identf = A.alloc([128], F32)
        onesf = A.alloc([128], F32)
        Ub = A.alloc([128], BF16)
        Lb = A.alloc([128], BF16)
        cosT = A.alloc([T], BF16)
        sinT = A.alloc([T], BF16)
        modTs = [A.alloc([96], F32) for _ in range(2)]
        badaTs = [A.alloc([96], F32) for _ in range(2)]
        rowt = [A.alloc([512], F32, npart=1) for _ in range(2)]
        cols = A.alloc([64], F32)
        s1c, s2c = cols[:, 0:16], cols[:, 16:32]
        condc = A.alloc([16], F32)
        condb = A.alloc([16], BF16)
        prm = A.alloc([16 * 6 + 96 + 8 + 80 + 20 + 8], F32)
        o_ = 0

        def take(n):
            nonlocal o_
            v = prm[:, o_:o_ + n]
            o_ += n
            return v
        n1g, n2g, fng, dcol, sng, badaT = take(16), take(16), take(16), take(16), take(16), take(96)
        _unused = take(16)
        sinkb, cbias, convb = take(8), take(80), take(20)
        misc = take(8)
        cqn, ckn, actxb, negflag, alogc, dtbc, negA = (misc[:, i:i + 1] for i in range(7))
        convw = A.alloc([20, 3], F32)
        nwf = A.alloc([20, 2], F32)
        kmul = A.alloc([8, 2], F32)
        sinkexp = A.alloc([8], F32)
        epsc = A.alloc([1], F32)

        ps = [nc.alloc_psum_tensor(f"ps{i}", [128, 512], F32) for i in range(8)]
        PB = [V(p, 0, 512, 0, 128, [(1, 512)]) for p in ps]
        PBh = [V(p.bitcast(BF16), 0, 1024, 0, 128, [(1, 1024)]) for p in ps]
        pk = [("ps", i) for i in range(8)]

        k_const = "const"
        self.dma(xT, xT_d.rearrange("(c p) t -> p c t", p=128), (), ["xT"])
        for v_, d_ in ((identf, ident_d), (condc, cond_d), (fng, fng_d), (cbias, cbias_d), (actxb, actxb_d),
                       (negflag, negflag_d), (kmul, kmul_d)):
            self.dma(v_, d_, (), [k_const])
        for v_, d_ in ((identb, ident_d), (rotb, rot_d), (Ub, U_d), (Lb, L_d), (cosT, cos_d), (sinT, sin_d)):
            self.dma(v_, d_, (), [k_const], q="pool")
        self.memset(onesb, 1.0, [k_const])
        self.memset(onesf, 1.0, [k_const])
        self.memset(epsc, EPS, [k_const])
        self.act(condb, condc, AF.Silu, [k_const], ["condb"])

        wstate = {"n": 0}

        def load_strip(Wd, r0, nrows, c0, ncols):
            kc = nrows // 128
            assert kc * ncols <= WS, (kc, ncols)
            i = wstate["n"] % len(wslots)
            wstate["n"] += 1
            sl = wslots[i]
            view = V(sl.t, sl.off, sl.pstride, 0, 128, [(ncols, kc), (1, ncols)], sl.aid)
            src = Wd[r0:r0 + nrows, c0:c0 + ncols].rearrange("(c p) n -> p c n", p=128)
            self.dma(view, src, (), (), q="pool")
            return view

        def add_slots(maxn=4):
            n = max(0, min(maxn, (A.nbytes - A.top - 64) // (WS * 2)))
            ex = [A.alloc([WS], BF16) for _ in range(n)]
            wslots.extend(ex)
            return ex

        def drop_slots(ex):
            for e_ in ex:
                wslots.remove(e_)

        rot = {"n": 0}

        def bank(choices=(0, 1, 2, 3)):
            b = choices[rot["n"] % len(choices)]
            rot["n"] += 1
            return b

        def linear(Wd, r0, nrows, c0, tiles, src, evac, halves=(0, 1), banks=(0, 1, 2, 3), tw=128):
            kc = nrows // 128
            per = max(1, min(len(tiles), WS // (kc * tw)))
            for g0 in range(0, len(tiles), per):
                grp = tiles[g0:g0 + per]
                lo = grp[0]
                hi = grp[-1] + tw
                strip = load_strip(Wd, r0, nrows, c0 + lo, hi - lo)
                for ti, tc in enumerate(grp):
                    for h in halves:
                        b = bank(banks)
                        pairs, rk = [], []
                        for k_ in range(kc):
                            rv, rkey = src(k_, h)
                            pairs.append((strip[:, k_, tc - lo:tc - lo + tw], rv))
                            if rkey is not None:
                                rk.append(rkey)
                        self.mm(PB[b][0:tw, :], pairs, rk, [pk[b]])
                        pend_step()
                        r_ = evac(g0 + ti, h, PB[b][0:tw, :], pk[b])
                        if r_ is not None:
                            try:
                                next(r_)
                                pend.append(r_)
                            except StopIteration:
                                pass
                bg_step()

        pend = []

        def pend_step():
            for g_ in list(pend):
                try:
                    next(g_)
                except StopIteration:
                    pend.remove(g_)

        def pend_drain():
            while pend:
                pend_step()

        bg = []
        bgcfg = {"rowbanks": (0, 1)}

        def bg_step(n=1):
            for _ in range(n):
                if not bg:
                    return
                try:
                    next(bg[0])
                except StopIteration:
                    bg.pop(0)

        def bg_drain():
            while bg:
                bg_step()

        def mod_gen(l, pairs):
            mT, bT = modTs[l % 2], badaTs[l % 2]
            for p in pairs:
                b = bank(bgcfg["rowbanks"])
                r_ = rowt[p % 2]
                for s_i in (2 * p, 2 * p + 1):
                    strip = load_strip(w_ada_d[l], 0, D, s_i * 256, 256)
                    self.mm(PB[b][0:1, (s_i % 2) * 256:(s_i % 2 + 1) * 256],
                            [(condb[:, k_:k_ + 1], strip[:, k_, :]) for k_ in range(16)], ["condb"], [pk[b]])
                self.copy(r_, PB[b][0:1, :], [pk[b]])
                b2 = bank((6, 7))
                for i in range(4):
                    self.mm1(PB[b2][:, i:i + 1], r_[0:1, i * 128:(i + 1) * 128], onesf[0:1, 0:1], True, True,
                             [k_const], [pk[b2]])
                self.tt(mT[:, 4 * p:4 * p + 4], PB[b2][:, 0:4], bT[:, 4 * p:4 * p + 4], ALU.add, [pk[b2], ("bada", l)],
                        [("modT", l, p)])
                yield

        def hsrc(k_, h):
            return hT[:, k_, h * 512:(h + 1) * 512], ("hT", k_, h)

        def HS(h):
            return slice(h * 512, (h + 1) * 512)

        def norm_stats(srcs, nsrc, dim, out_rstd, okey):
            sq = [A.alloc([512], BF16) for _ in range(2)]
            lnv = A.alloc([512], F32)
            for h in (0, 1):
                b = bank((4, 5))
                for i in range(nsrc):
                    sv, skey = srcs(i, h)
                    s_ = sq[i % 2]
                    self.act(s_, sv, AF.Square, [skey] if skey else [])
                    self.mm1(PB[b], onesb, s_, i == 0, i == nsrc - 1, [k_const], [pk[b]])
                self.act(lnv, PB[b], AF.Ln, [pk[b]], bias=epsc, scale=1.0 / dim)
                self.act(out_rstd[:, HS(h)], lnv, AF.Exp, [], [(okey, h)], scale=-0.5)

        def norm_mod(scale_c, shift_c, lkey):
            m = A.mark()
            rstd = A.alloc([T], F32)
            tmp = [A.alloc([512], F32) for _ in range(2)]
            norm_stats(lambda i, h: (xT[:, i, HS(h)], "xT"), 16, D, rstd, "rstd")
            for h in (0, 1):
                for c in range(16):
                    t_ = tmp[c % 2]
                    self.tt(t_, xT[:, c, HS(h)], rstd[:, HS(h)], ALU.mult, ["xT", ("rstd", h)])
                    self.act(hT[:, c, HS(h)], t_, AF.Identity, list(lkey) if isinstance(lkey, list) else [lkey], [("hT", c, h)],
                             bias=shift_c[:, c:c + 1], scale=scale_c[:, c:c + 1])
            A.release(m)

        def dbg_dump(name, view, reads, rows):
            if name not in dbg_o:
                return
            m = A.mark()
            n = int(np.prod([d_[1] for d_ in view.dims]))
            flat = V(view.t, view.off, view.pstride, 0, 128, [(1, n)], view.aid)
            CH = 2048
            st_ = [A.alloc([CH], F32) for _ in range(2)]
            for i, c0 in enumerate(range(0, n, CH)):
                self.copy(st_[i % 2], flat[:, c0:c0 + CH], reads)
                self.dma(dbg_o[name][:, c0:c0 + CH], st_[i % 2], [], [("dbg", name, i)])
            A.release(m)

        ISQ = 1.0 / math.sqrt(128.0)

        def attn_phase(l, mx):
            m_phase = A.mark()
            if mx == "a":
                oq, ok, ov, og = O_QA, O_KA, O_VA, O_GA
                kctx_d, vctx_d, k_o, v_o, w_o, bidx = akT_ctx_d, av_ctx_d, ak_o, av_o, w_oa_d, 0
            else:
                oq, ok, ov, og = O_QC, O_KC, O_VC, O_GC
                kctx_d, vctx_d, k_o, v_o, w_o, bidx = ckT_ctx_d, cv_ctx_d, ck_o, cv_o, w_oc_d, 2
            yst = A.alloc([8, T], BF16)
            m_att = A.mark()
            qst = A.alloc([8, 8, 128], BF16)
            kst = A.alloc([2, 1280], BF16)
            vtk = A.alloc([2, 10, 128], BF16)
            if mx == "a":
                amp = A.alloc([8, 128], BF16)
                amn = A.alloc([8, 128], BF16)
                self.dma(amp, amprev_d, q="pool")
                self.dma(amn, amnext_d, q="pool")
                self.act(sinkexp, sinkb, AF.Exp, [LK])
            for g in range(2):
                self.dma(kst[:, g, 0:256], kctx_d[l, g], q="pool")
                self.dma(vtk[:, g, 0:2, :], vctx_d[l, g].rearrange("(b p) d -> p b d", p=128), q="pool")
            m_proj = A.mark()
            qb = [A.alloc([512], BF16) for _ in range(2)]
            t1 = A.alloc([512], F32)
            t2 = A.alloc([512], F32)
            stg = [A.alloc([512], F32) for _ in range(2)]
            sqb = A.alloc([512], BF16)
            lnv = A.alloc([512], F32)
            rsq = A.alloc([512], F32)
            ex_slots = add_slots()
            cnt = {"n": 0}

            def qk_evac(kind, idx):
                gcol = cqn if kind == "q" else ckn

                def ev(ti, h, pv, bkey):
                    n_ = cnt["n"]
                    cnt["n"] += 1
                    q_b = qb[n_ % 2]
                    s_ = stg[n_ % 2]
                    hd = idx + ti
                    if mx == "c":
                        self.act(sqb, pv, AF.Square, [bkey])
                        yield
                        self.mm1(PB[6], onesb, sqb, True, True, [k_const], [pk[6]])
                        self.act(lnv, PB[6], AF.Ln, [pk[6]], bias=epsc, scale=1.0 / 128)
                        self.act(rsq, lnv, AF.Exp, scale=-0.5)
                        self.stt(s_, pv, gcol, rsq, ALU.mult, ALU.mult, [bkey, LK])
                    else:
                        self.copy(s_, pv, [bkey], eng="act")
                    src = s_
                    srck = []
                    if kind == "k":
                        self.dma(k_o[l, hd, :, HS(h)], s_, [], [("ko", mx, l, hd, h)])
                    self.copy(q_b, src, srck, eng="act")
                    yield
                    self.mm1(PB[7], rotb, q_b, True, True, [k_const], [pk[7]])
                    self.tt(t1, src, cosT[:, HS(h)], ALU.mult, srck + [k_const])
                    self.tt(t2, PB[7], sinT[:, HS(h)], ALU.mult, [pk[7], k_const])
                    if kind == "q":
                        dst = qst[:, 4 * h:4 * h + 4, hd, :]
                        self.tt(dst, t1.split(0, 128), t2.split(0, 128), ALU.add, [], [("qst", hd, h)])
                    else:
                        dst = kst[:, hd, 256 + h * 512:256 + (h + 1) * 512]
                        self.tt(dst, t1, t2, ALU.add, [], [("kst", hd)])
                return ev

            def v_evac(ti, h, pv, bkey):
                n_ = cnt["n"]
                cnt["n"] += 1
                s_ = stg[n_ % 2]
                q_b = qb[n_ % 2]
                self.copy(s_, pv, [bkey], eng="act")
                self.dma(v_o[l, ti, :, HS(h)], s_, [], [("vo", mx, l, ti, h)])
                self.copy(q_b, s_)
                yield
                for j in range(4):
                    self.transpose(PBh[7][:, j * 128:(j + 1) * 128], q_b[:, j * 128:(j + 1) * 128], identb,
                                   [k_const], [pk[7]])
                self.copy(vtk[:, ti, 2 + 4 * h:2 + 4 * h + 4, :], PBh[7][:, 0:512].split(0, 128), [pk[7]], [("vtk", ti)])

            linear(w_in_d[l], 0, D, oq, [i * 128 for i in range(8)], hsrc, qk_evac("q", 0))
            linear(w_in_d[l], 0, D, ok, [0, 128], hsrc, qk_evac("k", 0))
            linear(w_in_d[l], 0, D, ov, [0, 128], hsrc, v_evac)
            pend_drain()
            drop_slots(ex_slots)
            A.release(m_proj)
            pts = [A.alloc([512], BF16) for _ in range(3)]
            den = A.alloc([512], F32)
            lnd = A.alloc([512], F32)
            rec = A.alloc([512], F32)
            steps = []
            it = 0
            for i in range(8):
                for g in range(2):
                    OB, ZB = ((2, 3), (4, 5))[it % 2]
                    it += 1
                    kbs = []
                    if mx == "a":
                        kbs.append((kst[:, g, 0:128], vtk[:, g, 0, :], actxb, None))
                        kbs.append((kst[:, g, 128:256], vtk[:, g, 1, :], actxb, None))
                        if i > 0:
                            kbs.append((kst[:, g, 256 + (i - 1) * 128:256 + i * 128], vtk[:, g, 2 + i - 1, :], None, amp[:, i, :]))
                        kbs.append((kst[:, g, 256 + i * 128:256 + (i + 1) * 128], vtk[:, g, 2 + i, :], None, None))
                        if i < 7:
                            kbs.append((kst[:, g, 256 + (i + 1) * 128:256 + (i + 2) * 128], vtk[:, g, 2 + i + 1, :], None, amn[:, i, :]))
                    else:
                        for kb in range(10):
                            kbs.append((kst[:, g, kb * 128:(kb + 1) * 128], vtk[:, g, kb, :],
                                        cbias[:, i * 10 + kb:i * 10 + kb + 1], None))
                    for n_, kbt in enumerate(kbs):
                        steps.append((i, g, OB, ZB, n_, len(kbs), kbt))

            def emit_qk(si):
                i, g, OB, ZB, n_, nk, (kv_, vv_, bias_, mask_) = steps[si]
                sb = si % 2
                qv = qst[:, i, 4 * g:4 * g + 4, :]
                qkeys = [("qst", 4 * g + hh, i // 4) for hh in range(4)]
                self.mm1(PB[sb], kv_, qv, True, True, qkeys + [("kst", g)], [pk[sb]])

            ex_in = add_slots()
            bgcfg["rowbanks"] = (6, 7)
            emit_qk(0)
            for si in range(len(steps)):
                if si % 4 == 3:
                    bg_step()
                i, g, OB, ZB, n_, nk, (kv_, vv_, bias_, mask_) = steps[si]
                sb = si % 2
                if si + 1 < len(steps):
                    emit_qk(si + 1)
                pt = pts[si % 3]
                if bias_ is not None:
                    self.act(pt, PB[sb], AF.Exp, [pk[sb], k_const], bias=bias_, scale=ISQ)
                else:
                    self.act(pt, PB[sb], AF.Exp, [pk[sb]], scale=ISQ)
                if mask_ is not None:
                    self.tt(pt.split(0, 128), pt.split(0, 128), mask_.bc(0, 4), ALU.mult)
                first, last = n_ == 0, n_ == nk - 1
                self.mm1(PB[OB], vv_, pt, first, last, [("vtk", g)], [pk[OB]])
                self.mm1(PB[ZB], onesb, pt, first, last, [k_const], [pk[ZB]])
                if last:
                    if mx == "a":
                        self.tt(den.split(0, 128), PB[ZB].split(0, 128), sinkexp[:, 4 * g:4 * g + 4].bc(1, 128), ALU.add,
                                [pk[ZB]])
                        self.act(lnd, den, AF.Ln)
                    else:
                        self.act(lnd, PB[ZB], AF.Ln, [pk[ZB]])
                    self.act(rec, lnd, AF.Exp, scale=-1.0)
                    self.tt(yst[:, 4 * g:4 * g + 4, i * 128:(i + 1) * 128], PB[OB].split(0, 128), rec.split(0, 128), ALU.mult,
                            [pk[OB]], [("yst", i // 4)])
            bgcfg["rowbanks"] = (0, 1)
            drop_slots(ex_in)
            A.release(m_att)
            if ("y" + mx) in dbg_o and l == 0:
                dbg_dump("y" + mx, yst, [("yst", 0), ("yst", 1)], 128)
            merge_branch(l, og, w_o, 1024, lambda k_, h: (yst[:, k_, HS(h)], ("yst", h)), bidx, None)
            A.release(m_phase)

        def merge_branch(l, og, w_o, krows, ysrc, bidx, rstd_bc):
            m = A.mark()
            gbuf = A.alloc([4, T], BF16)
            mt = [A.alloc([512], BF16) for _ in range(2)]
            tf = A.alloc([512], F32)
            ex_slots = add_slots()
            cnt = {"n": 0}
            for grp in range(4):
                def ev_g(ti, h, pv, bkey):
                    self.act(gbuf[:, ti, HS(h)], pv, AF.Sigmoid, [bkey], [("gbuf", ti, h)])

                def ev_b(ti, h, pv, bkey, grp=grp):
                    n_ = cnt["n"]
                    cnt["n"] += 1
                    m_ = mt[n_ % 2]
                    if rstd_bc is not None:
                        self.tt(tf, pv, rstd_bc[:, HS(h)], ALU.mult, [bkey, ("rstdB", h)])
                        self.tt(m_, tf, gbuf[:, ti, HS(h)], ALU.mult, [("gbuf", ti, h)])
                    else:
                        self.tt(m_, pv, gbuf[:, ti, HS(h)], ALU.mult, [bkey, ("gbuf", ti, h)])
                    r0 = (grp * 4 + ti) * 128
                    self.dma(mrg_s[bidx, r0:r0 + 128, HS(h)], m_, [], [("mrg", bidx, grp * 4 + ti, h)])
                linear(w_in_d[l], 0, D, og + grp * 512, [0, 128, 256, 384], hsrc, ev_g)
                linear(w_o[l], 0, krows, grp * 512, [0, 128, 256, 384], ysrc, ev_b)
            drop_slots(ex_slots)
            A.release(m)

        def conv_tile(uraw, acc, ci, out, outkey):
            w0, w1, w2 = convw[:, ci, 0:1], convw[:, ci, 1:2], convw[:, ci, 2:3]
            self.act(acc, uraw[:, 1:1025], AF.Identity, [LK], bias=convb[:, ci:ci + 1], scale=w1)
            self.stt(acc, uraw[:, 0:1024], w0, acc, ALU.mult, ALU.add, [LK])
            self.stt(acc, uraw[:, 2:1026], w2, acc, ALU.mult, ALU.add, [LK])
            self.stt(acc[:, 256:1024:256], uraw[:, 256:1024:256], nwf[:, ci, 0:1], acc[:, 256:1024:256], ALU.mult, ALU.add, ["nwf"])
            self.stt(acc[:, 255:1023:256], uraw[:, 257:1025:256], nwf[:, ci, 1:2], acc[:, 255:1023:256], ALU.mult, ALU.add, ["nwf"])
            self.act(out, acc, AF.Silu, [], [outkey] if outkey else [])

        def ssd_phase(l):
            m_phase = A.mark()
            rstdB = A.alloc([T], F32)
            lnvB = A.alloc([512], F32)
            m_shared = A.mark()
            cumT = A.alloc([T], F32)
            BT = A.alloc([2, T], BF16)
            CT = A.alloc([2, T], BF16)
            Btok = A.alloc([8, 2, 128], BF16)
            CBm = A.alloc([2, T], BF16)
            S1 = A.alloc([8, 64], F32)
            S2 = A.alloc([8, 64], F32)
            S3 = A.alloc([8, 64], F32)
            cumtok = A.alloc([8, 64], F32)
            decbc = A.alloc([8, 64], F32)
            uraw = A.alloc([1026], F32)
            acc = A.alloc([T], F32)
            self.memset(uraw[:, 0:1], 0.0)
            self.memset(uraw[:, 1025:1026], 0.0)
            self.ts(nwf[:, :, 0], convw[:, :, 0], negflag, ALU.mult, [LK, k_const], ["nwf"])
            self.ts(nwf[:, :, 1], convw[:, :, 2], negflag, ALU.mult, [LK, k_const], ["nwf"])
            m_tmp = A.mark()
            dtT = A.alloc([T], F32)
            aT = A.alloc([T], F32)
            cmask = A.alloc([T], F32)
            et = A.alloc([512], F32)
            atot = A.alloc([8, 64], F32)
            toend = A.alloc([8, 64], F32)
            fmul = A.alloc([8, 64], F32)
            dtok_tmp = A.alloc([8, 64], F32)
            R64 = slice(0, 64)

            def dt_evac(ti, h, pv, bkey):
                self.act(et[R64], pv, AF.Exp, [bkey, LK], bias=dtbc[R64])
                self.act(dtT[R64, HS(h)], et[R64], AF.Ln, bias=1.0)
            linear(w_in_d[l], 0, D, O_DT, [0], hsrc, dt_evac, tw=64)
            self.act(negA[R64], alogc[R64], AF.Exp, [LK])
            self.ts(negA[R64], negA[R64], -1.0, ALU.mult)
            self.ts(aT[R64], dtT[R64], negA[R64], ALU.mult)
            self.memset(cmask[R64], 1.0)
            self.memset(cmask[R64, 0:1024:128], 0.0)
            self.scan(cumT[R64], cmask[R64], aT[R64], 0.0, ALU.mult, ALU.add)
            lastc = cumT[32:64, 127:1024:128].bc(1, 128)
            self.tt(cmask[32:64].split(0, 128), lastc, cumT[32:64].split(0, 128), ALU.subtract)
            self.tt(cumT[32:64], cmask[32:64], aT[32:64], ALU.add)
            for c in range(8):
                for srcT, dst in ((dtT, S1), (cumT, cumtok), (aT, dtok_tmp)):
                    b = bank((6, 7))
                    self.transpose(PB[b][:, 0:64], srcT[R64, c * 128:(c + 1) * 128], identf[R64, 0:64], [k_const], [pk[b]])
                    self.copy(dst[:, c, :], PB[b][:, 0:64], [pk[b]])
            b = bank((6, 7))
            for c in range(8):
                self.mm1(PB[b][:, c * 64:(c + 1) * 64], onesf, dtok_tmp[:, c, :], True, True, [k_const], [pk[b]])
            self.copy(atot, PB[b].split(0, 64), [pk[b]])
            self.tt(toend, atot, cumtok, ALU.subtract)
            self.act(toend, toend, AF.Exp)
            self.act(dtok_tmp, atot, AF.Exp)
            for d_ in range(2):
                self.tt(decbc[:, :, d_ * 32:(d_ + 1) * 32], dtok_tmp[:, :, d_ * 32:(d_ + 1) * 32],
                        kmul[:, :, d_].bc(1, 32), ALU.mult, [k_const])
            self.memset(fmul, 1.0)
            self.copy(fmul[:, 0:8:2, 0:32], dtok_tmp[:, 1:8:2, 0:32])
            self.copy(fmul[:, 1:8:2, 32:64], dtok_tmp[:, 0:8:2, 32:64])
            self.tt(S3, S1, toend, ALU.mult)
            for d_ in range(2):
                self.tt(S2[:, :, d_ * 32:(d_ + 1) * 32], S3[:, :, d_ * 32:(d_ + 1) * 32], kmul[:, :, d_].bc(1, 32), ALU.mult, [k_const])
            self.tt(S3, S3, fmul, ALU.mult)
            A.release(m_tmp)
            halfbuf = {}

            def bc_evac(dstT, ci0):
                def ev(ti, h, pv, bkey):
                    self.copy(uraw[:, 1 + h * 512:1 + (h + 1) * 512], pv, [bkey], eng="act")
                    if h == 1:
                        conv_tile(uraw, acc, ci0 + ti, dstT[:, ti, :], None)
                return ev
            linear(w_in_d[l], 0, D, O_B, [0, 128], hsrc, bc_evac(BT, 16))
            linear(w_in_d[l], 0, D, O_C, [0, 128], hsrc, bc_evac(CT, 18))
            for g in range(2):
                for c0 in (0, 4):
                    b = bank((6, 7))
                    for j in range(4):
                        c = c0 + j
                        self.transpose(PBh[b][:, j * 128:(j + 1) * 128], BT[:, g, c * 128:(c + 1) * 128], identb, [k_const], [pk[b]])
                    self.copy(Btok[:, c0:c0 + 4, g, :], PBh[b][:, 0:512].split(0, 128), [pk[b]])

            def make_cbm(g):
                for c0 in (0, 4):
                    b2 = bank((4, 5)) if False else bank((0, 1))
                    for j in range(4):
                        c = c0 + j
                        self.mm1(PB[b2][:, j * 128:(j + 1) * 128], BT[:, g, c * 128:(c + 1) * 128], CT[:, g, c * 128:(c + 1) * 128],
                                 True, True, [], [pk[b2]])
                    self.tt(CBm[:, 0, c0 * 128:(c0 + 4) * 128].split(0, 128), PB[b2].split(0, 128), Ub.bc(0, 4), ALU.mult, [pk[b2], k_const])
                    self.tt(CBm[:, 1, c0 * 128:(c0 + 4) * 128].split(0, 128), PB[b2].split(0, 128), Lb.bc(0, 4), ALU.mult, [pk[b2], k_const])
            m_tile = A.mark()
            zs = A.alloc([T], BF16)
            xc = A.alloc([T], BF16)
            xdt = A.alloc([2, 8, 128], BF16)
            xdw = A.alloc([8, 128], BF16)
            xfn = A.alloc([8, 128], BF16)
            hE = A.alloc([2, 8, 128], BF16)
            hm = [A.alloc([2, 128], F32) for _ in range(2)]
            h0s = A.alloc([2, 128], F32)
            dsegs = [A.alloc([512], F32) for _ in range(2)]
            segs = [A.alloc([512], BF16) for _ in range(2)]
            MC = [A.alloc([4, 512], BF16) for _ in range(2)]
            Mts = [[MC[hh_][:, 0], MC[hh_][:, 1]] for hh_ in range(2)]
            fss = [A.alloc([512], BF16) for _ in range(2)]
            Cfs = [[MC[hh_][:, 2], MC[hh_][:, 3]] for hh_ in range(2)]
            dseg, seg, fs, Mt = dsegs[0], segs[0], fss[0], Mts[0]
            tq_alias = V(A.t32, MC[1].off // 2, A.nbytes // 4, 0, 128, [(1, 512)], MC[1].aid)
            tq = tq_alias
            yg = A.alloc([512], F32)
            sqy = MC[1][:, 2]
            ygb = [MC[1][:, 3], MC[1][:, 3]]
            stf = [yg, dsegs[1]]
            nyg = 0
            nprep = {"n": 0}
            SQB = (4, 5)
            for j in range(16):
                g = j // 8
                if j % 8 == 0:
                    make_cbm(g)

                def z_evac(ti, h, pv, bkey):
                    self.act(zs[:, HS(h)], pv, AF.Silu, [bkey])

                def x_evac(ti, h, pv, bkey):
                    self.copy(uraw[:, 1 + h * 512:1 + (h + 1) * 512], pv, [bkey], eng="act")
                if j == 0:
                    linear(w_in_d[l], 0, D, O_XBC, [0], hsrc, x_evac, banks=(0, 1))
                linear(w_in_d[l], 0, D, O_Z + j * 128, [0], hsrc, z_evac, banks=(0, 1))
                conv_tile(uraw, acc, j, xc, None)
                xbk = {}
                for c0 in (0, 4):
                    b = bank((6, 7))
                    xbk[c0] = b
                    for jj in range(4):
                        c = c0 + jj
                        self.transpose(PBh[b][:, jj * 128:(jj + 1) * 128], xc[:, c * 128:(c + 1) * 128], identb, [k_const], [pk[b]])
                stb = {}
                for d_ in range(2):
                    for c0 in (0, 4):
                        b = xbk[c0]
                        xps = PBh[b][:, 0:512].split(0, 128).split(1, 64)
                        for dst, S in ((xdt[:, d_], S1), (xdw, S2), (xfn, S3)):
                            sc = S[:, c0:c0 + 4, d_ * 32 + 2 * j:d_ * 32 + 2 * j + 2].bc(2, 64)
                            self.tt(dst[:, c0:c0 + 4, :].split(1, 64), xps, sc, ALU.mult, [pk[b]])
                    for c0 in (0, 4):
                        b = 2 * d_ + c0 // 4
                        stb[(d_, c0)] = b
                        for jj in range(4):
                            c = c0 + jj
                            self.mm1(PB[b][:, jj * 128:(jj + 1) * 128], Btok[:, c, g, :], xdw[:, c, :], True, True, [], [pk[b]])
                    b = bank((4, 5))
                    for sq_ in range(4):
                        self.mm(PB[b][:, sq_ * 128:(sq_ + 1) * 128],
                                [(xfn[:, 2 * sq_, :], Btok[:, 2 * sq_, g, :]), (xfn[:, 2 * sq_ + 1, :], Btok[:, 2 * sq_ + 1, g, :])],
                                [], [pk[b]])
                    self.copy(stf[d_], PB[b], [pk[b]], eng="act")
                    self.dma(st_o[l, :, d_, j * 128:(j + 1) * 128, :].rearrange("s p n -> p s n"), stf[d_].split(0, 128), [],
                             [("sto", l, d_, j)])
                for d_ in range(2):
                    self.dma(h0s[:, d_, :], h0T_d[l, d_, :, j * 128:(j + 1) * 128])
                for d_ in range(2):
                    order = list(range(8)) if d_ == 0 else list(range(7, -1, -1))
                    self.copy(hE[:, d_, order[0], :], h0s[:, d_, :], eng="act")
                    prev = h0s[:, d_, :]
                    for n_ in range(7):
                        c = order[n_]
                        cn = order[n_ + 1]
                        b = stb[(d_, (c // 4) * 4)]
                        dcv = decbc[:, c, d_ * 32 + 2 * j:d_ * 32 + 2 * j + 2].bc(1, 64)
                        cur = hm[n_ % 2][:, d_, :]
                        self.tt(cur.split(0, 64), prev.split(0, 64), dcv, ALU.mult)
                        self.tt(cur, cur, PB[b][:, (c % 4) * 128:(c % 4 + 1) * 128], ALU.add, [pk[b]])
                        self.copy(hE[:, d_, cn, :], cur, eng="act")
                        prev = cur
                if j + 1 < 16:
                    linear(w_in_d[l], 0, D, O_XBC + (j + 1) * 128, [0], hsrc, x_evac, banks=(0, 1))
                blocks = [(h, hh) for h in (0, 1) for hh in (0, 1)]
                ybank = {0: bank((2, 3)), 1: None}
                ybank[1] = 5 - ybank[0]
                selb = {}

                def emit_sel(bi):
                    h, hh = blocks[bi]
                    for d_ in range(2):
                        row = d_ * 32 + 2 * j + hh
                        rb = bank((6, 7))
                        selb[(bi, d_)] = rb
                        sel = V(identf.t, identf.off + row, identf.pstride, 0, 64, [(0, 128)], identf.aid)
                        self.mm1(PB[rb], sel, cumT[R64, HS(h)], True, True, [k_const], [pk[rb]])

                def it_bufs(it):
                    return dsegs[it % 2], segs[it % 2], fss[it % 2]

                def stage1(it):
                    bi, d_ = it // 2, it % 2
                    h, hh = blocks[bi]
                    row = d_ * 32 + 2 * j + hh
                    dseg_, seg_, fs_ = it_bufs(it)
                    rb = selb[(bi, d_)]
                    self.tt(dseg_.split(0, 128), PB[rb].split(0, 128), cumtok[:, 4 * h:4 * h + 4, row].bc(1, 128),
                            ALU.subtract, [pk[rb]])
                    self.ts(dseg_, dseg_, 0.0, ALU.min)
                    self.act(seg_, dseg_, AF.Exp)
                    self.act(fs_, PB[rb], AF.Exp, [pk[rb], ("b", dseg_.aid)])

                def stage2(it):
                    bi, d_ = it // 2, it % 2
                    h, hh = blocks[bi]
                    dseg_, seg_, fs_ = it_bufs(it)
                    self.tt(Mts[hh][d_], seg_, CBm[:, d_, HS(h)], ALU.mult)
                    self.tt(Cfs[hh][d_], CT[:, g, HS(h)], fs_, ALU.mult)

                emit_sel(0)
                stage1(0)
                for it in range(8):
                    bi, d_ = it // 2, it % 2
                    h, hh = blocks[bi]
                    yb = ybank[h]
                    if d_ == 1 and bi + 1 < len(blocks):
                        emit_sel(bi + 1)
                    if it + 1 < 8:
                        stage1(it + 1)
                    stage2(it)
                    if d_ == 0:
                        continue
                    for cc in range(4):
                        c = 4 * h + cc
                        out = PB[yb][hh * 64:(hh + 1) * 64, cc * 128:(cc + 1) * 128]
                        pairs = []
                        for dd in range(2):
                            pairs.append((xdt[:, dd, c, hh * 64:(hh + 1) * 64], Mts[hh][dd][:, cc * 128:(cc + 1) * 128]))
                            pairs.append((hE[:, dd, c, hh * 64:(hh + 1) * 64], Cfs[hh][dd][:, cc * 128:(cc + 1) * 128]))
                        self.mm(out, pairs, [], [pk[yb]])
                    if hh == 0:
                        continue
                    self.stt(tq, xc[:, HS(h)], dcol[:, j:j + 1], PB[yb], ALU.mult, ALU.add, [pk[yb], LK])
                    self.tt(yg, tq, zs[:, HS(h)], ALU.mult)
                    self.act(sqy, yg, AF.Square)
                    sb_ = bank((4, 5))
                    self.mm1(PB[sb_], onesb, sqy, True, True, [k_const], [pk[sb_]])
                    if j == 0:
                        self.copy(rstdB[:, HS(h)], PB[sb_], [pk[sb_]], [("ssq", h)])
                    else:
                        self.tt(rstdB[:, HS(h)], rstdB[:, HS(h)], PB[sb_], ALU.add, [pk[sb_], ("ssq", h)], [("ssq", h)])
                    y_b = ygb[nyg % 2]
                    nyg += 1
                    self.act(y_b, yg, AF.Identity, [LK], scale=sng[:, j:j + 1])
                    self.dma(ygs_s[j * 128:(j + 1) * 128, HS(h)], y_b, [], [("ygs", j, h)])
            A.release(m_shared)
            lnv = lnvB
            for h in (0, 1):
                self.act(lnv, rstdB[:, HS(h)], AF.Ln, [("ssq", h)], bias=epsc, scale=1.0 / D)
                self.act(rstdB[:, HS(h)], lnv, AF.Exp, [], [("rstdB", h)], scale=-0.5)
            ygT = A.alloc([16, T], BF16)
            for j in range(16):
                for h in (0, 1):
                    self.dma(ygT[:, j, HS(h)], ygs_s[j * 128:(j + 1) * 128, HS(h)], [("ygs", j, h)], [("ygT", j, h)])
            if "yb" in dbg_o and l == 0:
                dbg_dump("yb", ygT, [("ygT", j, h) for j in range(16) for h in (0, 1)], 128)
            merge_branch(l, O_GB, w_ob_d, D, lambda k_, h: (ygT[:, k_, HS(h)], ("ygT", k_, h)), 1, rstdB)
            A.release(m_phase)

        for l in range(NL):
            LK = ("lp", l)
            self.P.tag = f"L{l}.mod"
            for v_, d_ in ((n1g, n1g_d[l]), (n2g, n2g_d[l]), (dcol, dcol_d[l]), (sng, sng_d[l]),
                           (sinkb, sink_d[l]), (convb, convb_d[l]), (cqn, cqn_d[l]), (ckn, ckn_d[l]),
                           (alogc[0:64], alog_d[l]), (dtbc[0:64], dtb_d[l])):
                self.dma(v_, d_, (), [LK])
            self.dma(convw, convw_d[l], (), [LK])
            modT, badaT_l = modTs[l % 2], badaTs[l % 2]
            if l == 0:
                m_mod0 = A.mark()
                self.dma(badaTs[0], badaT_d[0], (), [("bada", 0)])
                ex0 = add_slots()
                for _ in mod_gen(0, range(0, 8)):
                    pass
                drop_slots(ex0)
                A.release(m_mod0)
                bg.append(mod_gen(0, range(8, 24)))
            else:
                bg_drain()
            MODK = [("modT", l, p) for p in range(24)]
            self.stt(s1c, modT[:, 16:32], 1.0, n1g, ALU.add, ALU.mult, [("modT", l, p) for p in range(4, 8)] + [LK], ["cols"])
            sh1, g1c, sh2, g2c = modT[:, 0:16], modT[:, 32:48], modT[:, 48:64], modT[:, 80:96]
            MK = "cols"
            self.P.tag = f"L{l}.norm1"
            norm_mod(s1c, sh1, [MK] + [("modT", l, p) for p in range(0, 4)])
            ph = self.phases
            if ph is None or "a" in ph:
                self.P.tag = f"L{l}.attnA"
                attn_phase(l, "a")
            if l + 1 < NL:
                self.dma(badaTs[(l + 1) % 2], badaT_d[l + 1], (), [("bada", l + 1)])
                bg.append(mod_gen(l + 1, range(0, 24)))
            if ph is None or "c" in ph:
                self.P.tag = f"L{l}.attnC"
                attn_phase(l, "c")
            if ph is None or "b" in ph:
                self.P.tag = f"L{l}.ssd"
                ssd_phase(l)
            self.P.tag = f"L{l}.out"
            if ph is not None and "out" not in ph:
                continue
            bg_drain()
            self.stt(s2c, modT[:, 64:80], 1.0, n2g, ALU.add, ALU.mult, [("modT", l, p) for p in range(16, 20)] + [LK], ["cols2"])
            m = A.mark()
            mrgT = A.alloc([16, T], BF16)
            mtmp = [A.alloc([512], BF16) for _ in range(12)]
            ex_slots = add_slots()
            nn = 0
            for k_ in range(16):
                for h in (0, 1):
                    self.dma(mrgT[:, k_, HS(h)], mrg_s[0, k_ * 128:(k_ + 1) * 128, HS(h)], [("mrg", 0, k_, h)], [("mrgT", k_, h)])
                    for bi in (1, 2):
                        t_ = mtmp[nn % 12]
                        nn += 1
                        self.dma(t_, mrg_s[bi, k_ * 128:(k_ + 1) * 128, HS(h)], [("mrg", bi, k_, h)])
                        self.tt(mrgT[:, k_, HS(h)], mrgT[:, k_, HS(h)], t_, ALU.add, [("mrgT", k_, h)], [("mrgT", k_, h)])

            def out_evac(ti, h, pv, bkey):
                self.stt(xT[:, ti, HS(h)], pv, g1c[:, ti:ti + 1], xT[:, ti, HS(h)], ALU.mult, ALU.add, [bkey, "xT"] + [("modT", l, p) for p in range(8, 12)], ["xT"])
            linear(w_out_d[l], 0, D, 0, [i * 128 for i in range(16)], lambda k_, h: (mrgT[:, k_, HS(h)], ("mrgT", k_, h)), out_evac)
            drop_slots(ex_slots)
            A.release(m)
            if "x1" in dbg_o and l == 0:
                dbg_dump("x1", xT, ["xT"], 128)
            self.P.tag = f"L{l}.mlp"
            norm_mod(s2c, sh2, ["cols2"] + [("modT", l, p) for p in range(12, 16)])
            m = A.mark()
            f1 = A.alloc([16, T], BF16)
            rl = [A.alloc([512], F32) for _ in range(2)]
            ex_slots = add_slots()
            nr = {"n": 0}
            for fb in range(4):
                def f1_evac(ti, h, pv, bkey):
                    r_ = rl[nr["n"] % 2]
                    nr["n"] += 1
                    self.act(r_, pv, AF.Relu, [bkey])
                    self.tt(f1[:, ti, HS(h)], pv, r_, ALU.mult, [bkey], [("f1", ti, h)])

                def f2_evac(ti, h, pv, bkey):
                    self.stt(xT[:, ti, HS(h)], pv, g2c[:, ti:ti + 1], xT[:, ti, HS(h)], ALU.mult, ALU.add, [bkey, "xT"] + [("modT", l, p) for p in range(20, 24)], ["xT"])
                linear(w_m1_d[l], 0, D, fb * 2048, [i * 128 for i in range(16)], hsrc, f1_evac)
                linear(w_m2_d[l], fb * 2048, 2048, 0, [i * 128 for i in range(16)],
                       lambda k_, h: (f1[:, k_, HS(h)], ("f1", k_, h)), f2_evac)
            drop_slots(ex_slots)
            A.release(m)

        self.P.tag = "final"
        m = A.mark()
        rstd = A.alloc([T], F32)
        norm_stats(lambda i, h: (xT[:, i, HS(h)], "xT"), 16, D, rstd, "rstdF")
        stg = [A.alloc([512], F32) for _ in range(2)]
        n = 0
        for h in (0, 1):
            for c in range(16):
                s_ = stg[n % 2]
                n += 1
                self.stt(s_, xT[:, c, HS(h)], fng[:, c:c + 1], rstd[:, HS(h)], ALU.mult, ALU.mult,
                         ["xT", ("rstdF", h), k_const])
                self.dma(yT_o[c * 128:(c + 1) * 128, HS(h)], s_, [], [("yT", c, h)])
        A.release(m)
        self.P.emit(nc)
        return nc


def _rope_tables():
    rows = T // 64
    row = np.repeat(np.arange(rows, dtype=np.float32), 64)
    col = np.tile(np.arange(64, dtype=np.float32), rows)
    n_freq = 32
    inv = (10000.0 ** (-np.arange(n_freq, dtype=np.float32) / n_freq)).astype(np.float32)
    ang = np.concatenate([row[:, None] * inv, col[:, None] * inv], axis=-1)
    return np.cos(ang).astype(np.float32), np.sin(ang).astype(np.float32)


def make_in_maps(inp):
    f = np.float32
    cos, sin = _rope_tables()
    cosT_s = np.ascontiguousarray(np.concatenate([cos, cos], axis=1).T)
    sinT_s = np.ascontiguousarray(np.concatenate([-sin, sin], axis=1).T)
    ident = np.eye(128, dtype=f)
    rotm = np.zeros((128, 128), f)
    for dp in range(128):
        rotm[(dp + 64) % 128, dp] = 1.0
    U = np.triu(np.ones((128, 128), f))
    L = np.tril(np.ones((128, 128), f))

    def colT(v):
        return np.ascontiguousarray(np.asarray(v, f).reshape(-1, 128).T)

    shared = {
        "n1g": np.stack([colT(inp["norm1_g"][l]) for l in range(DEPTH)]),
        "n2g": np.stack([colT(inp["norm2_g"][l]) for l in range(DEPTH)]),
        "fng": colT(inp["final_norm_g"]),
        "badaT": np.stack([colT(inp["b_ada"][l]) for l in range(DEPTH)]),
        "sinkbc": np.ascontiguousarray(np.broadcast_to(np.asarray(inp["a_sink"], f)[:, None, :], (DEPTH, 128, 8))),
        "convw": np.stack([np.ascontiguousarray(np.asarray(inp["conv_w"][l], f).T.reshape(20, 128, 3).transpose(1, 0, 2))
                           for l in range(DEPTH)]),
        "convb": np.stack([colT(inp["conv_b"][l]) for l in range(DEPTH)]),
        "alog": np.asarray(inp["ssm_a_log"], f).reshape(DEPTH, 64, 1),
        "dtb": np.asarray(inp["ssm_dt_bias"], f).reshape(DEPTH, 64, 1),
        "dcol": np.stack([colT(np.repeat(np.asarray(inp["ssm_d"][l], f), 64)) for l in range(DEPTH)]),
        "sng": np.stack([colT(inp["ssm_norm_g"][l]) for l in range(DEPTH)]),
        "cqn": np.asarray(inp["c_q_norm"], f).reshape(DEPTH, 128, 1),
        "ckn": np.asarray(inp["c_k_norm"], f).reshape(DEPTH, 128, 1),
        "ident": ident, "rot": rotm, "U": U, "L": L,
    }
    for k_ in ("w_ada", "w_in", "w_oa", "w_ob", "w_oc", "w_out", "w_mlp1", "w_mlp2"):
        shared[k_] = np.asarray(inp[k_], f)
    tri_prev = np.triu(np.ones((128, 128), f))
    tri_prev = np.ascontiguousarray(tri_prev.T)
    tri_next = np.ascontiguousarray(np.triu(np.ones((128, 128), f)))
    maps = []
    for core in range(8):
        m = dict(shared)
        if core < 4:
            b = core
            m["xT"] = np.ascontiguousarray(np.asarray(inp["x_sample"][b], f).T)
            m["cond"] = colT(inp["c"][b])
            m["cosT"], m["sinT"] = cosT_s, sinT_s
            m["akT_ctx"] = np.ascontiguousarray(np.asarray(inp["cache_a_k"][b], f).transpose(0, 2, 3, 1))
            m["av_ctx"] = np.ascontiguousarray(np.asarray(inp["cache_a_v"][b], f).transpose(0, 2, 1, 3))
            m["ckT_ctx"] = np.ascontiguousarray(np.asarray(inp["cache_c_k"][b], f).transpose(0, 2, 3, 1))
            m["cv_ctx"] = np.ascontiguousarray(np.asarray(inp["cache_c_v"][b], f).transpose(0, 2, 1, 3))
            m["h0T"] = np.ascontiguousarray(np.asarray(inp["state_ssm"][b], f).reshape(DEPTH, 2, 2048, 128).transpose(0, 1, 3, 2))
            amprev = np.broadcast_to(tri_prev[:, None, :], (128, 8, 128)).copy()
            amnext = np.broadcast_to(tri_next[:, None, :], (128, 8, 128)).copy()
            m["actxb"] = np.zeros((128, 1), f)
            m["cbias"] = np.zeros((128, 80), f)
            m["kmul"] = np.ones((128, 8, 2), f)
            m["negflag"] = np.zeros((128, 1), f)
        else:
            j = core - 4
            xp = np.asarray(inp["x_prompt"][4 * j:4 * j + 4], f).reshape(T, D)
            m["xT"] = np.ascontiguousarray(xp.T)
            m["cond"] = colT(inp["c_ctx"])
            m["cosT"], m["sinT"] = np.ones((128, T), f), np.zeros((128, T), f)
            for k_ in ("akT_ctx", "ckT_ctx"):
                m[k_] = np.zeros((DEPTH, 2, 128, 256), f)
            for k_ in ("av_ctx", "cv_ctx"):
                m[k_] = np.zeros((DEPTH, 2, 256, 128), f)
            m["h0T"] = np.zeros((DEPTH, 2, 128, 2048), f)
            amprev = np.zeros((128, 8, 128), f)
            amnext = np.zeros((128, 8, 128), f)
            amprev[:, 1::2, :] = 1.0
            amnext[:, 0::2, :] = 1.0
            m["actxb"] = np.full((128, 1), NEG, f)
            cb = np.full((8, 10), NEG, f)
            for i in range(8):
                s = i // 2
                cb[i, 2 + 2 * s] = 0.0
                cb[i, 2 + 2 * s + 1] = 0.0
            m["cbias"] = np.ascontiguousarray(np.broadcast_to(cb.reshape(1, 80), (128, 80)))
            km = np.zeros((8, 2), f)
            km[0::2, 0] = 1.0
            km[1::2, 1] = 1.0
            m["kmul"] = np.ascontiguousarray(np.broadcast_to(km[None], (128, 8, 2)))
            m["negflag"] = np.full((128, 1), -1.0, f)
        m["amprev"], m["amnext"] = amprev, amnext
        maps.append(m)
    return maps


_CACHE = {}


def kernel(**inputs):
    if "k" not in _CACHE:
        kb = K()
        kb.build()
        _CACHE["k"] = kb
    kb = _CACHE["k"]
    maps = make_in_maps(inputs)
    maps = [{k_: v for k_, v in m.items() if k_ in kb.inputs} for m in maps]
    res = run_bass_kernel_spmd(kb.nc, maps, core_ids=list(range(8)))
    R = res.results
    f = np.float32
    y_sample = np.stack([R[b]["yT"].T for b in range(4)]).astype(f)
    y_prompt = np.concatenate([R[4 + j]["yT"].T.reshape(4, 256, D) for j in range(4)]).astype(f)

    def cache(name):
        outs = []
        for j in range(4):
            a = R[4 + j][name]
            a = a.reshape(DEPTH, 2, 128, 4, 256)
            outs.append(a.transpose(3, 0, 4, 1, 2))
        return np.ascontiguousarray(np.concatenate(outs)).astype(f)

    st = np.concatenate([R[4 + j]["st_o"].transpose(1, 0, 2, 3, 4) for j in range(4)])
    st = np.ascontiguousarray(st.reshape(16, DEPTH, 2, 32, 64, 128)).astype(f)
    return (y_prompt, y_sample, cache("akT_o"), cache("avT_o"), cache("ckT_o"), cache("cvT_o"), st)
```

```python
import math
import numpy as np
import concourse.bass as bass
import concourse.mybir as mybir
from concourse.bass_utils import run_bass_kernel_spmd

F32 = mybir.dt.float32
BF16 = mybir.dt.bfloat16
AF = mybir.ActivationFunctionType
ALU = mybir.AluOpType

D = 2048
T = 1024
NKC = 16
DEPTH = 2
EPS = 1e-6
NEG = -30000.0
O_QA, O_KA, O_VA, O_Z, O_XBC, O_DT, O_QC, O_KC, O_VC, O_GA, O_GB, O_GC = (
    0, 1024, 1280, 1536, 3584, 6144, 6208, 7232, 7488, 7744, 9792, 11840)
O_B = O_XBC + 2048
O_C = O_XBC + 2048 + 256
N_IN = 13888


class V:
    def __init__(self, t, off, pstride, p0, npart, dims, aid=None):
        self.t, self.off, self.pstride, self.p0, self.np, self.dims = t, off, pstride, p0, npart, list(dims)
        self.aid = aid

    def __getitem__(self, idx):
        if not isinstance(idx, tuple):
            idx = (idx,)
        idx = list(idx) + [slice(None)] * (1 + len(self.dims) - len(idx))
        ps = idx[0]
        p0, npart = self.p0, self.np
        if isinstance(ps, slice):
            a = 0 if ps.start is None else ps.start
            b = self.np if ps.stop is None else ps.stop
            p0, npart = self.p0 + a, b - a
        else:
            p0, npart = self.p0 + ps, 1
        off = self.off
        dims = []
        for (st, n), ix in zip(self.dims, idx[1:]):
            if isinstance(ix, slice):
                a = 0 if ix.start is None else ix.start
                b = n if ix.stop is None else ix.stop
                step = 1 if ix.step is None else ix.step
                cnt = (b - a + step - 1) // step
                off += a * st
                dims.append((st * step, cnt))
            else:
                off += ix * st
        return V(self.t, off, self.pstride, p0, npart, dims, self.aid)

    def bc(self, pos, n):
        d = list(self.dims)
        d.insert(pos, (0, n))
        return V(self.t, self.off, self.pstride, self.p0, self.np, d, self.aid)

    def split(self, pos, inner):
        st, n = self.dims[pos]
        d = list(self.dims)
        d[pos:pos + 1] = [(st * inner, n // inner), (st, inner)]
        return V(self.t, self.off, self.pstride, self.p0, self.np, d, self.aid)

    @property
    def ap(self):
        dims = [[st, n] for st, n in self.dims]
        out = []
        for st, n in dims:
            if out and out[-1][0] == st * n and st != 0:
                out[-1] = [st, out[-1][1] * n]
            else:
                out.append([st, n])
        if not out:
            out = [[1, 1]]
        return bass.AP(self.t, self.off + self.p0 * self.pstride, [[self.pstride, self.np]] + out)


class Arena:
    def __init__(self, nc, name, nbytes):
        self.t16 = nc.alloc_sbuf_tensor(name, [128, nbytes // 2], BF16)
        self.t32 = self.t16.bitcast(F32)
        self.nbytes = nbytes
        self.top = 0
        self.peak = 0
        self.live = []
        self.retired = []
        self.prior = {}
        self.naid = 0

    def alloc(self, dims, dtype, npart=128):
        n = int(np.prod(dims))
        esz = 4 if dtype == F32 else 2
        self.top = (self.top + 31) // 32 * 32
        boff = self.top
        self.top += n * esz
        self.peak = max(self.peak, self.top)
        assert self.top <= self.nbytes, f"arena overflow {self.top} > {self.nbytes}"
        strides = []
        s = 1
        for d_ in reversed(dims):
            strides.append(s)
            s *= d_
        strides = strides[::-1]
        t = self.t32 if dtype == F32 else self.t16
        self.naid += 1
        aid = self.naid
        end = boff + n * esz
        pr = [a for (a, s0, e0) in self.retired if s0 < end and boff < e0]
        if pr:
            self.prior[aid] = pr
        self.live.append((aid, boff, end))
        return V(t, boff // esz, self.nbytes // esz, 0, npart, list(zip(strides, dims)), aid)

    def mark(self):
        return self.top

    def release(self, m):
        self.top = m
        keep = []
        for ent in self.live:
            if ent[1] >= m:
                self.retired.append(ent)
            else:
                keep.append(ent)
        self.live = keep


class Op:
    __slots__ = ("eng", "fn", "deps", "signal", "count", "sem", "semval", "dma", "tag", "nins")


class Prog:
    ENGS = ("pe", "act", "dve", "pool", "sp")

    def __init__(self):
        self.ops = []
        self.lastw = {}
        self.readers = {}
        self.ndma = {"pool": 0, "sp": 0, "act": 0}
        self.users = {}
        self.arena = None

    def add(self, eng, fn, reads=(), writes=(), dma=False, aids_r=(), aids_w=()):
        op = Op()
        op.eng, op.fn, op.signal, op.count, op.dma = eng, fn, False, 0, dma
        op.sem, op.semval = None, 0
        op.tag, op.nins = getattr(self, "tag", ""), 1
        deps = []
        for r in reads:
            w = self.lastw.get(r)
            if w is not None:
                deps.append(w)
        for w_ in writes:
            w = self.lastw.get(w_)
            if w is not None:
                deps.append(w)
            deps.extend(self.readers.get(w_, ()))
        for r in reads:
            self.readers.setdefault(r, []).append(op)
        for w_ in writes:
            self.lastw[w_] = op
            self.readers[w_] = []
        for a in list(aids_w) + list(aids_r):
            if a is None:
                continue
            for p in self.arena.prior.get(a, ()):
                u = self.users.get(p)
                if u:
                    deps.extend(u["eng"].values())
                    deps.extend(u["dma"])
        for a in list(aids_w) + list(aids_r):
            if a is None:
                continue
            u = self.users.setdefault(a, {"eng": {}, "dma": []})
            if dma:
                u["dma"].append(op)
            else:
                u["eng"][eng] = op
        seen = set()
        op.deps = []
        for d_ in deps:
            if d_ is op or id(d_) in seen:
                continue
            seen.add(id(d_))
            if d_.eng == "pe" and eng == "pe" and not d_.dma:
                continue
            op.deps.append(d_)
            d_.signal = True
        self.ops.append(op)
        return op

    def emit(self, nc, final_wait_eng="sp"):
        NDS = 16
        engsem = {e: nc.alloc_semaphore(f"s_{e}") for e in ("pe", "act", "dve", "pool")}
        dmasem = {q: [nc.alloc_semaphore(f"d_{q}{i}") for i in range(NDS)] for q in ("pool", "sp", "act")}
        dmacnt = {q: [0] * NDS for q in dmasem}
        cnt = {e: 0 for e in engsem}
        rr = {q: 0 for q in dmasem}
        for op in self.ops:
            if op.dma:
                q = op.eng
                k = rr[q] % NDS
                rr[q] += 1
                dmacnt[q][k] += 1
                op.sem, op.semval = dmasem[q][k], 16 * dmacnt[q][k]
            else:
                if op.signal:
                    cnt[op.eng] += 1
                    op.sem, op.semval = engsem[op.eng], cnt[op.eng]
        byeng = {e: [o for o in self.ops if o.eng == e] for e in self.ENGS}
        handles = {"pe": "tensor", "act": "scalar", "dve": "vector", "pool": "gpsimd", "sp": "sync"}
        finals = []
        for q in dmasem:
            for k in range(NDS):
                if dmacnt[q][k]:
                    finals.append((dmasem[q][k], 16 * dmacnt[q][k]))

        def run(engname, e):
            waited = {}
            for op in byeng[engname]:
                for d_ in op.deps:
                    key = d_.sem.num
                    if waited.get(key, 0) < d_.semval:
                        e.wait_ge(d_.sem, d_.semval)
                        waited[key] = d_.semval
                if op.dma and op.semval > 16:
                    key = op.sem.num
                    if waited.get(key, 0) < op.semval - 16:
                        e.wait_ge(op.sem, op.semval - 16)
                        waited[key] = op.semval - 16
                inst = op.fn(e)
                if op.dma:
                    inst.then_inc(op.sem, 16)
                elif op.signal:
                    inst.then_inc(op.sem, 1)
            if engname == final_wait_eng:
                for sem, val in finals:
                    e.wait_ge(sem, val)
                for en, c in cnt.items():
                    if c:
                        e.wait_ge(engsem[en], c)

        with nc.Block() as block:
            @block.tensor
            def _(e):
                run("pe", e)

            @block.scalar
            def _(e):
                run("act", e)

            @block.vector
            def _(e):
                run("dve", e)

            @block.gpsimd
            def _(e):
                run("pool", e)

            @block.sync
            def _(e):
                run("sp", e)


class K:
    def __init__(self, dbg=None, nlayers=DEPTH, phases=None, skip=()):
        self.dbg = dbg or {}
        self.skip = set(skip)
        self.nlayers = nlayers
        self.phases = phases
        self.nc = bass.Bass("TRN2", target_bir_lowering=False)
        self.P = Prog()
        self.inputs = {}
        self.outputs = {}
        self.uid = 0

    def din(self, name, shape):
        if name in self.skip:
            shape = [1, 1]
        t = self.nc.dram_tensor(name, list(shape), F32, kind="ExternalInput")
        self.inputs[name] = tuple(shape)
        return t.ap()

    def dout(self, name, shape):
        t = self.nc.dram_tensor(name, list(shape), F32, kind="ExternalOutput")
        self.outputs[name] = tuple(shape)
        return t.ap()

    def dscratch(self, name, shape, dtype):
        return self.nc.dram_tensor(name, list(shape), dtype, kind="Internal").ap()

    def key(self, base):
        self.uid += 1
        return (base, self.uid)

    def _add(self, eng, fn, outs, ins, reads, writes, dma=False):
        aw = [v.aid for v in outs if isinstance(v, V)]
        ar = [v.aid for v in ins if isinstance(v, V)]
        reads = list(reads) + [("b", a) for a in ar if a is not None]
        writes = list(writes) + [("b", a) for a in aw if a is not None]
        return self.P.add(eng, fn, reads, writes, dma=dma, aids_r=ar, aids_w=aw)

    def act(self, out, in_, func, reads=(), writes=(), bias=None, scale=None):
        kw = {}
        if bias is not None:
            kw["bias"] = bias.ap if isinstance(bias, V) else bias
        if scale is not None:
            kw["scale"] = scale.ap if isinstance(scale, V) else scale
        o, i = out.ap, in_.ap
        self._add("act", lambda e: e.activation(out=o, in_=i, func=func, **kw), [out], [in_, bias, scale], reads, writes)

    def tt(self, out, a, b, op, reads=(), writes=(), eng="dve"):
        o, x, y = out.ap, a.ap, b.ap
        self._add(eng, lambda e: e.tensor_tensor(out=o, in0=x, in1=y, op=op), [out], [a, b], reads, writes)

    def ts(self, out, a, s1, op0, reads=(), writes=(), s2=None, op1=None, eng="dve"):
        o, x = out.ap, a.ap
        s1a = s1.ap if isinstance(s1, V) else s1
        s2a = s2.ap if isinstance(s2, V) else s2
        if op1 is None:
            fn = lambda e: e.tensor_scalar(out=o, in0=x, scalar1=s1a, scalar2=None, op0=op0)
        else:
            fn = lambda e: e.tensor_scalar(out=o, in0=x, scalar1=s1a, scalar2=s2a, op0=op0, op1=op1)
        self._add(eng, fn, [out], [a, s1, s2], reads, writes)

    def stt(self, out, a, s, b, op0, op1, reads=(), writes=()):
        o, x, y = out.ap, a.ap, b.ap
        sa = s.ap if isinstance(s, V) else s
        self._add("dve", lambda e: e.scalar_tensor_tensor(out=o, in0=x, scalar=sa, in1=y, op0=op0, op1=op1),
                  [out], [a, s, b], reads, writes)

    def copy(self, out, in_, reads=(), writes=(), eng="dve"):
        o, i = out.ap, in_.ap
        if eng == "act":
            self._add("act", lambda e: e.activation(out=o, in_=i, func=AF.Copy), [out], [in_], reads, writes)
        else:
            self._add(eng, lambda e: e.tensor_copy(out=o, in_=i), [out], [in_], reads, writes)

    def memset(self, out, val, writes=(), eng="dve"):
        o = out.ap
        self._add(eng, lambda e: e.memset(o, val), [out], [], (), writes)

    def scan(self, out, d0, d1, init, op0, op1, reads=(), writes=()):
        o, a, b = out.ap, d0.ap, d1.ap
        self._add("dve", lambda e: e.tensor_tensor_scan(out=o, data0=a, data1=b, initial=init, op0=op0, op1=op1),
                  [out], [d0, d1], reads, writes)

    def mm(self, out, pairs, reads=(), writes=()):
        o = out.ap
        pr = [(l.ap, r.ap) for l, r in pairs]

        def fn(e):
            inst = None
            for i, (l, r) in enumerate(pr):
                inst = e.matmul(o, l, r, start=(i == 0), stop=(i == len(pr) - 1))
            return inst
        self._add("pe", fn, [out], [v for p in pairs for v in p], reads, writes).nins = len(pr)

    def mm1(self, out, lhsT, rhs, start, stop, reads=(), writes=()):
        o, l, r = out.ap, lhsT.ap, rhs.ap
        self._add("pe", lambda e: e.matmul(o, l, r, start=start, stop=stop), [out], [lhsT, rhs], reads, writes)

    def transpose(self, out, in_, ident, reads=(), writes=()):
        o, i, d_ = out.ap, in_.ap, ident.ap
        self._add("pe", lambda e: e.transpose(o, i, d_), [out], [in_, ident], reads, writes)

    def dma(self, out, in_, reads=(), writes=(), q="sp"):
        o = out.ap if isinstance(out, V) else out
        i = in_.ap if isinstance(in_, V) else in_
        self._add(q, lambda e: e.dma_start(out=o, in_=i), [out], [in_], reads, writes, dma=True)

    def build(self):
        nc = self.nc
        NL = self.nlayers
        xT_d = self.din("xT", [D, T])
        cond_d = self.din("cond", [128, 16])
        cos_d = self.din("cosT", [128, T])
        sin_d = self.din("sinT", [128, T])
        akT_ctx_d = self.din("akT_ctx", [DEPTH, 2, 128, 256])
        av_ctx_d = self.din("av_ctx", [DEPTH, 2, 256, 128])
        ckT_ctx_d = self.din("ckT_ctx", [DEPTH, 2, 128, 256])
        cv_ctx_d = self.din("cv_ctx", [DEPTH, 2, 256, 128])
        h0T_d = self.din("h0T", [DEPTH, 2, 128, 2048])
        amprev_d = self.din("amprev", [128, 8, 128])
        amnext_d = self.din("amnext", [128, 8, 128])
        actxb_d = self.din("actxb", [128, 1])
        cbias_d = self.din("cbias", [128, 80])
        kmul_d = self.din("kmul", [128, 8, 2])
        negflag_d = self.din("negflag", [128, 1])
        n1g_d = self.din("n1g", [DEPTH, 128, 16])
        n2g_d = self.din("n2g", [DEPTH, 128, 16])
        fng_d = self.din("fng", [128, 16])
        badaT_d = self.din("badaT", [DEPTH, 128, 96])
        sink_d = self.din("sinkbc", [DEPTH, 128, 8])
        convw_d = self.din("convw", [DEPTH, 128, 20, 3])
        convb_d = self.din("convb", [DEPTH, 128, 20])
        alog_d = self.din("alog", [DEPTH, 64, 1])
        dtb_d = self.din("dtb", [DEPTH, 64, 1])
        dcol_d = self.din("dcol", [DEPTH, 128, 16])
        sng_d = self.din("sng", [DEPTH, 128, 16])
        cqn_d = self.din("cqn", [DEPTH, 128, 1])
        ckn_d = self.din("ckn", [DEPTH, 128, 1])
        w_ada_d = self.din("w_ada", [DEPTH, D, 6 * D])
        w_in_d = self.din("w_in", [DEPTH, D, N_IN])
        w_oa_d = self.din("w_oa", [DEPTH, 1024, D])
        w_ob_d = self.din("w_ob", [DEPTH, D, D])
        w_oc_d = self.din("w_oc", [DEPTH, 1024, D])
        w_out_d = self.din("w_out", [DEPTH, D, D])
        w_m1_d = self.din("w_mlp1", [DEPTH, D, 4 * D])
        w_m2_d = self.din("w_mlp2", [DEPTH, 4 * D, D])
        ident_d = self.din("ident", [128, 128])
        rot_d = self.din("rot", [128, 128])
        U_d = self.din("U", [128, 128])
        L_d = self.din("L", [128, 128])

        yT_o = self.dout("yT", [D, T])
        ak_o = self.dout("akT_o", [DEPTH, 2, 128, T])
        av_o = self.dout("avT_o", [DEPTH, 2, 128, T])
        ck_o = self.dout("ckT_o", [DEPTH, 2, 128, T])
        cv_o = self.dout("cvT_o", [DEPTH, 2, 128, T])
        st_o = self.dout("st_o", [DEPTH, 4, 2, 2048, 128])
        dbg_o = {n: self.dout("dbg_" + n, shp) for n, shp in self.dbg.items()}

        mrg_s = self.dscratch("mrg_s", [3, D, T], BF16)
        ygs_s = self.dscratch("ygs_s", [D, T], BF16)

        A = Arena(nc, "arena", 212480)
        self.A = A
        self.P.arena = A
        xT = A.alloc([16, T], F32)
        hT = A.alloc([16, T], BF16)
        WS = 4096
        NSLOT = 2
        wslots = [A.alloc([WS], BF16) for _ in range(NSLOT)]
        identb = A.alloc([128], BF16)
        onesb = A.alloc([128], BF16)
        rotb = A.alloc([128], BF16)
        identf = A.alloc([128], F32)
        onesf = A.alloc([128], F32)
        Ub = A.alloc([128], BF16)
        Lb = A.alloc([128], BF16)
        cosT = A.alloc([T], BF16)
        sinT = A.alloc([T], BF16)
        modTs = [A.alloc([96], F32) for _ in range(2)]
        badaTs = [A.alloc([96], F32) for _ in range(2)]
        rowt = [A.alloc([512], F32, npart=1) for _ in range(2)]
        cols = A.alloc([64], F32)
        s1c, s2c = cols[:, 0:16], cols[:, 16:32]
        condc = A.alloc([16], F32)
        condb = A.alloc([16], BF16)
        prm = A.alloc([16 * 6 + 96 + 8 + 80 + 20 + 8], F32)
        o_ = 0

        def take(n):
            nonlocal o_
            v = prm[:, o_:o_ + n]
            o_ += n
            return v
        n1g, n2g, fng, dcol, sng, badaT = take(16), take(16), take(16), take(16), take(16), take(96)
        _unused = take(16)
        sinkb, cbias, convb = take(8), take(80), take(20)
        misc = take(8)
        cqn, ckn, actxb, negflag, alogc, dtbc, negA = (misc[:, i:i + 1] for i in range(7))
        convw = A.alloc([20, 3], F32)
        nwf = A.alloc([20, 2], F32)
        kmul = A.alloc([8, 2], F32)
        sinkexp = A.alloc([8], F32)
        epsc = A.alloc([1], F32)

        ps = [nc.alloc_psum_tensor(f"ps{i}", [128, 512], F32) for i in range(8)]
        PB = [V(p, 0, 512, 0, 128, [(1, 512)]) for p in ps]
        PBh = [V(p.bitcast(BF16), 0, 1024, 0, 128, [(1, 1024)]) for p in ps]
        pk = [("ps", i) for i in range(8)]

        k_const = "const"
        self.dma(xT, xT_d.rearrange("(c p) t -> p c t", p=128), (), ["xT"])
        for v_, d_ in ((identf, ident_d), (condc, cond_d), (fng, fng_d), (cbias, cbias_d), (actxb, actxb_d),
                       (negflag, negflag_d), (kmul, kmul_d)):
            self.dma(v_, d_, (), [k_const])
        for v_, d_ in ((identb, ident_d), (rotb, rot_d), (Ub, U_d), (Lb, L_d), (cosT, cos_d), (sinT, sin_d)):
            self.dma(v_, d_, (), [k_const], q="pool")
        self.memset(onesb, 1.0, [k_const])
        self.memset(onesf, 1.0, [k_const])
        self.memset(epsc, EPS, [k_const])
        self.act(condb, condc, AF.Silu, [k_const], ["condb"])

        wstate = {"n": 0}

        def load_strip(Wd, r0, nrows, c0, ncols):
            kc = nrows // 128
            assert kc * ncols <= WS, (kc, ncols)
            i = wstate["n"] % len(wslots)
            wstate["n"] += 1
            sl = wslots[i]
            view = V(sl.t, sl.off, sl.pstride, 0, 128, [(ncols, kc), (1, ncols)], sl.aid)
            src = Wd[r0:r0 + nrows, c0:c0 + ncols].rearrange("(c p) n -> p c n", p=128)
            self.dma(view, src, (), (), q="pool")
            return view

        def add_slots(maxn=4):
            n = max(0, min(maxn, (A.nbytes - A.top - 64) // (WS * 2)))
            ex = [A.alloc([WS], BF16) for _ in range(n)]
            wslots.extend(ex)
            return ex

        def drop_slots(ex):
            for e_ in ex:
                wslots.remove(e_)

        rot = {"n": 0}

        def bank(choices=(0, 1, 2, 3)):
            b = choices[rot["n"] % len(choices)]
            rot["n"] += 1
            return b

        def linear(Wd, r0, nrows, c0, tiles, src, evac, halves=(0, 1), banks=(0, 1, 2, 3), tw=128):
            kc = nrows // 128
            per = max(1, min(len(tiles), WS // (kc * tw)))
            for g0 in range(0, len(tiles), per):
                grp = tiles[g0:g0 + per]
                lo = grp[0]
                hi = grp[-1] + tw
                strip = load_strip(Wd, r0, nrows, c0 + lo, hi - lo)
                for ti, tc in enumerate(grp):
                    for h in halves:
                        b = bank(banks)
                        pairs, rk = [], []
                        for k_ in range(kc):
                            rv, rkey = src(k_, h)
                            pairs.append((strip[:, k_, tc - lo:tc - lo + tw], rv))
                            if rkey is not None:
                                rk.append(rkey)
                        self.mm(PB[b][0:tw, :], pairs, rk, [pk[b]])
                        pend_step()
                        r_ = evac(g0 + ti, h, PB[b][0:tw, :], pk[b])
                        if r_ is not None:
                            try:
                                next(r_)
                                pend.append(r_)
                            except StopIteration:
                                pass
                bg_step()

        pend = []

        def pend_step():
            for g_ in list(pend):
                try:
                    next(g_)
                except StopIteration:
                    pend.remove(g_)

        def pend_drain():
            while pend:
                pend_step()

        bg = []
        bgcfg = {"rowbanks": (0, 1)}

        def bg_step(n=1):
            for _ in range(n):
                if not bg:
                    return
                try:
                    next(bg[0])
                except StopIteration:
                    bg.pop(0)

        def bg_drain():
            while bg:
                bg_step()

        def mod_gen(l, pairs):
            mT, bT = modTs[l % 2], badaTs[l % 2]
            for p in pairs:
                b = bank(bgcfg["rowbanks"])
                r_ = rowt[p % 2]
                for s_i in (2 * p, 2 * p + 1):
                    strip = load_strip(w_ada_d[l], 0, D, s_i * 256, 256)
                    self.mm(PB[b][0:1, (s_i % 2) * 256:(s_i % 2 + 1) * 256],
                            [(condb[:, k_:k_ + 1], strip[:, k_, :]) for k_ in range(16)], ["condb"], [pk[b]])
                self.copy(r_, PB[b][0:1, :], [pk[b]])
                b2 = bank((6, 7))
                for i in range(4):
                    self.mm1(PB[b2][:, i:i + 1], r_[0:1, i * 128:(i + 1) * 128], onesf[0:1, 0:1], True, True,
                             [k_const], [pk[b2]])
                self.tt(mT[:, 4 * p:4 * p + 4], PB[b2][:, 0:4], bT[:, 4 * p:4 * p + 4], ALU.add, [pk[b2], ("bada", l)],
                        [("modT", l, p)])
                yield

        def hsrc(k_, h):
            return hT[:, k_, h * 512:(h + 1) * 512], ("hT", k_, h)

        def HS(h):
            return slice(h * 512, (h + 1) * 512)

        def norm_stats(srcs, nsrc, dim, out_rstd, okey):
            sq = [A.alloc([512], BF16) for _ in range(2)]
            lnv = A.alloc([512], F32)
            for h in (0, 1):
                b = bank((4, 5))
                for i in range(nsrc):
                    sv, skey = srcs(i, h)
                    s_ = sq[i % 2]
                    self.act(s_, sv, AF.Square, [skey] if skey else [])
                    self.mm1(PB[b], onesb, s_, i == 0, i == nsrc - 1, [k_const], [pk[b]])
                self.act(lnv, PB[b], AF.Ln, [pk[b]], bias=epsc, scale=1.0 / dim)
                self.act(out_rstd[:, HS(h)], lnv, AF.Exp, [], [(okey, h)], scale=-0.5)

        def norm_mod(scale_c, shift_c, lkey):
            m = A.mark()
            rstd = A.alloc([T], F32)
            tmp = [A.alloc([512], F32) for _ in range(2)]
            norm_stats(lambda i, h: (xT[:, i, HS(h)], "xT"), 16, D, rstd, "rstd")
            for h in (0, 1):
                for c in range(16):
                    t_ = tmp[c % 2]
                    self.tt(t_, xT[:, c, HS(h)], rstd[:, HS(h)], ALU.mult, ["xT", ("rstd", h)])
                    self.act(hT[:, c, HS(h)], t_, AF.Identity, list(lkey) if isinstance(lkey, list) else [lkey], [("hT", c, h)],
                             bias=shift_c[:, c:c + 1], scale=scale_c[:, c:c + 1])
            A.release(m)

        def dbg_dump(name, view, reads, rows):
            if name not in dbg_o:
                return
            m = A.mark()
            n = int(np.prod([d_[1] for d_ in view.dims]))
            flat = V(view.t, view.off, view.pstride, 0, 128, [(1, n)], view.aid)
            CH = 2048
            st_ = [A.alloc([CH], F32) for _ in range(2)]
            for i, c0 in enumerate(range(0, n, CH)):
                self.copy(st_[i % 2], flat[:, c0:c0 + CH], reads)
                self.dma(dbg_o[name][:, c0:c0 + CH], st_[i % 2], [], [("dbg", name, i)])
            A.release(m)

        ISQ = 1.0 / math.sqrt(128.0)

        def attn_phase(l, mx):
            m_phase = A.mark()
            if mx == "a":
                oq, ok, ov, og = O_QA, O_KA, O_VA, O_GA
                kctx_d, vctx_d, k_o, v_o, w_o, bidx = akT_ctx_d, av_ctx_d, ak_o, av_o, w_oa_d, 0
            else:
                oq, ok, ov, og = O_QC, O_KC, O_VC, O_GC
                kctx_d, vctx_d, k_o, v_o, w_o, bidx = ckT_ctx_d, cv_ctx_d, ck_o, cv_o, w_oc_d, 2
            yst = A.alloc([8, T], BF16)
            m_att = A.mark()
            qst = A.alloc([8, 8, 128], BF16)
            kst = A.alloc([2, 1280], BF16)
            vtk = A.alloc([2, 10, 128], BF16)
            if mx == "a":
                amp = A.alloc([8, 128], BF16)
                amn = A.alloc([8, 128], BF16)
                self.dma(amp, amprev_d, q="pool")
                self.dma(amn, amnext_d, q="pool")
                self.act(sinkexp, sinkb, AF.Exp, [LK])
            for g in range(2):
                self.dma(kst[:, g, 0:256], kctx_d[l, g], q="pool")
                self.dma(vtk[:, g, 0:2, :], vctx_d[l, g].rearrange("(b p) d -> p b d", p=128), q="pool")
            m_proj = A.mark()
            qb = [A.alloc([512], BF16) for _ in range(2)]
            t1 = A.alloc([512], F32)
            t2 = A.alloc([512], F32)
            stg = [A.alloc([512], F32) for _ in range(2)]
            sqb = A.alloc([512], BF16)
            lnv = A.alloc([512], F32)
            rsq = A.alloc([512], F32)
            ex_slots = add_slots()
            cnt = {"n": 0}

            def qk_evac(kind, idx):
                gcol = cqn if kind == "q" else ckn

                def ev(ti, h, pv, bkey):
                    n_ = cnt["n"]
                    cnt["n"] += 1
                    q_b = qb[n_ % 2]
                    s_ = stg[n_ % 2]
                    hd = idx + ti
                    if mx == "c":
                        self.act(sqb, pv, AF.Square, [bkey])
                        yield
                        self.mm1(PB[6], onesb, sqb, True, True, [k_const], [pk[6]])
                        self.act(lnv, PB[6], AF.Ln, [pk[6]], bias=epsc, scale=1.0 / 128)
                        self.act(rsq, lnv, AF.Exp, scale=-0.5)
                        self.stt(s_, pv, gcol, rsq, ALU.mult, ALU.mult, [bkey, LK])
                    else:
                        self.copy(s_, pv, [bkey], eng="act")
                    src = s_
                    srck = []
                    if kind == "k":
                        self.dma(k_o[l, hd, :, HS(h)], s_, [], [("ko", mx, l, hd, h)])
                    self.copy(q_b, src, srck, eng="act")
                    yield
                    self.mm1(PB[7], rotb, q_b, True, True, [k_const], [pk[7]])
                    self.tt(t1, src, cosT[:, HS(h)], ALU.mult, srck + [k_const])
                    self.tt(t2, PB[7], sinT[:, HS(h)], ALU.mult, [pk[7], k_const])
                    if kind == "q":
                        dst = qst[:, 4 * h:4 * h + 4, hd, :]
                        self.tt(dst, t1.split(0, 128), t2.split(0, 128), ALU.add, [], [("qst", hd, h)])
                    else:
                        dst = kst[:, hd, 256 + h * 512:256 + (h + 1) * 512]
                        self.tt(dst, t1, t2, ALU.add, [], [("kst", hd)])
                return ev

            def v_evac(ti, h, pv, bkey):
                n_ = cnt["n"]
                cnt["n"] += 1
                s_ = stg[n_ % 2]
                q_b = qb[n_ % 2]
                self.copy(s_, pv, [bkey], eng="act")
                self.dma(v_o[l, ti, :, HS(h)], s_, [], [("vo", mx, l, ti, h)])
                self.copy(q_b, s_)
                yield
                for j in range(4):
                    self.transpose(PBh[7][:, j * 128:(j + 1) * 128], q_b[:, j * 128:(j + 1) * 128], identb,
                                   [k_const], [pk[7]])
                self.copy(vtk[:, ti, 2 + 4 * h:2 + 4 * h + 4, :], PBh[7][:, 0:512].split(0, 128), [pk[7]], [("vtk", ti)])

            linear(w_in_d[l], 0, D, oq, [i * 128 for i in range(8)], hsrc, qk_evac("q", 0))
            linear(w_in_d[l], 0, D, ok, [0, 128], hsrc, qk_evac("k", 0))
            linear(w_in_d[l], 0, D, ov, [0, 128], hsrc, v_evac)
            pend_drain()
            drop_slots(ex_slots)
            A.release(m_proj)
            pts = [A.alloc([512], BF16) for _ in range(3)]
            den = A.alloc([512], F32)
            lnd = A.alloc([512], F32)
            rec = A.alloc([512], F32)
            steps = []
            it = 0
            for i in range(8):
                for g in range(2):
                    OB, ZB = ((2, 3), (4, 5))[it % 2]
                    it += 1
                    kbs = []
                    if mx == "a":
                        kbs.append((kst[:, g, 0:128], vtk[:, g, 0, :], actxb, None))
                        kbs.append((kst[:, g, 128:256], vtk[:, g, 1, :], actxb, None))
                        if i > 0:
                            kbs.append((kst[:, g, 256 + (i - 1) * 128:256 + i * 128], vtk[:, g, 2 + i - 1, :], None, amp[:, i, :]))
                        kbs.append((kst[:, g, 256 + i * 128:256 + (i + 1) * 128], vtk[:, g, 2 + i, :], None, None))
                        if i < 7:
                            kbs.append((kst[:, g, 256 + (i + 1) * 128:256 + (i + 2) * 128], vtk[:, g, 2 + i + 1, :], None, amn[:, i, :]))
                    else:
                        for kb in range(10):
                            kbs.append((kst[:, g, kb * 128:(kb + 1) * 128], vtk[:, g, kb, :],
                                        cbias[:, i * 10 + kb:i * 10 + kb + 1], None))
                    for n_, kbt in enumerate(kbs):
                        steps.append((i, g, OB, ZB, n_, len(kbs), kbt))

            def emit_qk(si):
                i, g, OB, ZB, n_, nk, (kv_, vv_, bias_, mask_) = steps[si]
                sb = si % 2
                qv = qst[:, i, 4 * g:4 * g + 4, :]
                qkeys = [("qst", 4 * g + hh, i // 4) for hh in range(4)]
                self.mm1(PB[sb], kv_, qv, True, True, qkeys + [("kst", g)], [pk[sb]])

            ex_in = add_slots()
            bgcfg["rowbanks"] = (6, 7)
            emit_qk(0)
            for si in range(len(steps)):
                if si % 4 == 3:
                    bg_step()
                i, g, OB, ZB, n_, nk, (kv_, vv_, bias_, mask_) = steps[si]
                sb = si % 2
                if si + 1 < len(steps):
                    emit_qk(si + 1)
                pt = pts[si % 3]
                if bias_ is not None:
                    self.act(pt, PB[sb], AF.Exp, [pk[sb], k_const], bias=bias_, scale=ISQ)
                else:
                    self.act(pt, PB[sb], AF.Exp, [pk[sb]], scale=ISQ)
                if mask_ is not None:
                    self.tt(pt.split(0, 128), pt.split(0, 128), mask_.bc(0, 4), ALU.mult)
                first, last = n_ == 0, n_ == nk - 1
                self.mm1(PB[OB], vv_, pt, first, last, [("vtk", g)], [pk[OB]])
                self.mm1(PB[ZB], onesb, pt, first, last, [k_const], [pk[ZB]])
                if last:
                    if mx == "a":
                        self.tt(den.split(0, 128), PB[ZB].split(0, 128), sinkexp[:, 4 * g:4 * g + 4].bc(1, 128), ALU.add,
                                [pk[ZB]])
                        self.act(lnd, den, AF.Ln)
                    else:
                        self.act(lnd, PB[ZB], AF.Ln, [pk[ZB]])
                    self.act(rec, lnd, AF.Exp, scale=-1.0)
                    self.tt(yst[:, 4 * g:4 * g + 4, i * 128:(i + 1) * 128], PB[OB].split(0, 128), rec.split(0, 128), ALU.mult,
                            [pk[OB]], [("yst", i // 4)])
            bgcfg["rowbanks"] = (0, 1)
            drop_slots(ex_in)
            A.release(m_att)
            if ("y" + mx) in dbg_o and l == 0:
                dbg_dump("y" + mx, yst, [("yst", 0), ("yst", 1)], 128)
            merge_branch(l, og, w_o, 1024, lambda k_, h: (yst[:, k_, HS(h)], ("yst", h)), bidx, None)
            A.release(m_phase)

        def merge_branch(l, og, w_o, krows, ysrc, bidx, rstd_bc):
            m = A.mark()
            gbuf = A.alloc([4, T], BF16)
            mt = [A.alloc([512], BF16) for _ in range(2)]
            tf = A.alloc([512], F32)
            ex_slots = add_slots()
            cnt = {"n": 0}
            for grp in range(4):
                def ev_g(ti, h, pv, bkey):
                    self.act(gbuf[:, ti, HS(h)], pv, AF.Sigmoid, [bkey], [("gbuf", ti, h)])

                def ev_b(ti, h, pv, bkey, grp=grp):
                    n_ = cnt["n"]
                    cnt["n"] += 1
                    m_ = mt[n_ % 2]
                    if rstd_bc is not None:
                        self.tt(tf, pv, rstd_bc[:, HS(h)], ALU.mult, [bkey, ("rstdB", h)])
                        self.tt(m_, tf, gbuf[:, ti, HS(h)], ALU.mult, [("gbuf", ti, h)])
                    else:
                        self.tt(m_, pv, gbuf[:, ti, HS(h)], ALU.mult, [bkey, ("gbuf", ti, h)])
                    r0 = (grp * 4 + ti) * 128
                    self.dma(mrg_s[bidx, r0:r0 + 128, HS(h)], m_, [], [("mrg", bidx, grp * 4 + ti, h)])
                linear(w_in_d[l], 0, D, og + grp * 512, [0, 128, 256, 384], hsrc, ev_g)
                linear(w_o[l], 0, krows, grp * 512, [0, 128, 256, 384], ysrc, ev_b)
            drop_slots(ex_slots)
            A.release(m)

        def conv_tile(uraw, acc, ci, out, outkey):
            w0, w1, w2 = convw[:, ci, 0:1], convw[:, ci, 1:2], convw[:, ci, 2:3]
            self.act(acc, uraw[:, 1:1025], AF.Identity, [LK], bias=convb[:, ci:ci + 1], scale=w1)
            self.stt(acc, uraw[:, 0:1024], w0, acc, ALU.mult, ALU.add, [LK])
            self.stt(acc, uraw[:, 2:1026], w2, acc, ALU.mult, ALU.add, [LK])
            self.stt(acc[:, 256:1024:256], uraw[:, 256:1024:256], nwf[:, ci, 0:1], acc[:, 256:1024:256], ALU.mult, ALU.add, ["nwf"])
            self.stt(acc[:, 255:1023:256], uraw[:, 257:1025:256], nwf[:, ci, 1:2], acc[:, 255:1023:256], ALU.mult, ALU.add, ["nwf"])
            self.act(out, acc, AF.Silu, [], [outkey] if outkey else [])

        def ssd_phase(l):
            m_phase = A.mark()
            rstdB = A.alloc([T], F32)
            lnvB = A.alloc([512], F32)
            m_shared = A.mark()
            cumT = A.alloc([T], F32)
            BT = A.alloc([2, T], BF16)
            CT = A.alloc([2, T], BF16)
            Btok = A.alloc([8, 2, 128], BF16)
            CBm = A.alloc([2, T], BF16)
            S1 = A.alloc([8, 64], F32)
            S2 = A.alloc([8, 64], F32)
            S3 = A.alloc([8, 64], F32)
            cumtok = A.alloc([8, 64], F32)
            decbc = A.alloc([8, 64], F32)
            uraw = A.alloc([1026], F32)
            acc = A.alloc([T], F32)
            self.memset(uraw[:, 0:1], 0.0)
            self.memset(uraw[:, 1025:1026], 0.0)
            self.ts(nwf[:, :, 0], convw[:, :, 0], negflag, ALU.mult, [LK, k_const], ["nwf"])
            self.ts(nwf[:, :, 1], convw[:, :, 2], negflag, ALU.mult, [LK, k_const], ["nwf"])
            m_tmp = A.mark()
            dtT = A.alloc([T], F32)
            aT = A.alloc([T], F32)
            cmask = A.alloc([T], F32)
            et = A.alloc([512], F32)
            atot = A.alloc([8, 64], F32)
            toend = A.alloc([8, 64], F32)
            fmul = A.alloc([8, 64], F32)
            dtok_tmp = A.alloc([8, 64], F32)
            R64 = slice(0, 64)

            def dt_evac(ti, h, pv, bkey):
                self.act(et[R64], pv, AF.Exp, [bkey, LK], bias=dtbc[R64])
                self.act(dtT[R64, HS(h)], et[R64], AF.Ln, bias=1.0)
            linear(w_in_d[l], 0, D, O_DT, [0], hsrc, dt_evac, tw=64)
            self.act(negA[R64], alogc[R64], AF.Exp, [LK])
            self.ts(negA[R64], negA[R64], -1.0, ALU.mult)
            self.ts(aT[R64], dtT[R64], negA[R64], ALU.mult)
            self.memset(cmask[R64], 1.0)
            self.memset(cmask[R64, 0:1024:128], 0.0)
            self.scan(cumT[R64], cmask[R64], aT[R64], 0.0, ALU.mult, ALU.add)
            lastc = cumT[32:64, 127:1024:128].bc(1, 128)
            self.tt(cmask[32:64].split(0, 128), lastc, cumT[32:64].split(0, 128), ALU.subtract)
            self.tt(cumT[32:64], cmask[32:64], aT[32:64], ALU.add)
            for c in range(8):
                for srcT, dst in ((dtT, S1), (cumT, cumtok), (aT, dtok_tmp)):
                    b = bank((6, 7))
                    self.transpose(PB[b][:, 0:64], srcT[R64, c * 128:(c + 1) * 128], identf[R64, 0:64], [k_const], [pk[b]])
                    self.copy(dst[:, c, :], PB[b][:, 0:64], [pk[b]])
            b = bank((6, 7))
            for c in range(8):
                self.mm1(PB[b][:, c * 64:(c + 1) * 64], onesf, dtok_tmp[:, c, :], True, True, [k_const], [pk[b]])
            self.copy(atot, PB[b].split(0, 64), [pk[b]])
            self.tt(toend, atot, cumtok, ALU.subtract)
            self.act(toend, toend, AF.Exp)
            self.act(dtok_tmp, atot, AF.Exp)
            for d_ in range(2):
                self.tt(decbc[:, :, d_ * 32:(d_ + 1) * 32], dtok_tmp[:, :, d_ * 32:(d_ + 1) * 32],
                        kmul[:, :, d_].bc(1, 32), ALU.mult, [k_const])
            self.memset(fmul, 1.0)
            self.copy(fmul[:, 0:8:2, 0:32], dtok_tmp[:, 1:8:2, 0:32])
            self.copy(fmul[:, 1:8:2, 32:64], dtok_tmp[:, 0:8:2, 32:64])
            self.tt(S3, S1, toend, ALU.mult)
            for d_ in range(2):
                self.tt(S2[:, :, d_ * 32:(d_ + 1) * 32], S3[:, :, d_ * 32:(d_ + 1) * 32], kmul[:, :, d_].bc(1, 32), ALU.mult, [k_const])
            self.tt(S3, S3, fmul, ALU.mult)
            A.release(m_tmp)
            halfbuf = {}

            def bc_evac(dstT, ci0):
                def ev(ti, h, pv, bkey):
                    self.copy(uraw[:, 1 + h * 512:1 + (h + 1) * 512], pv, [bkey], eng="act")
                    if h == 1:
                        conv_tile(uraw, acc, ci0 + ti, dstT[:, ti, :], None)
                return ev
            linear(w_in_d[l], 0, D, O_B, [0, 128], hsrc, bc_evac(BT, 16))
            linear(w_in_d[l], 0, D, O_C, [0, 128], hsrc, bc_evac(CT, 18))
            for g in range(2):
                for c0 in (0, 4):
                    b = bank((6, 7))
                    for j in range(4):
                        c = c0 + j
                        self.transpose(PBh[b][:, j * 128:(j + 1) * 128], BT[:, g, c * 128:(c + 1) * 128], identb, [k_const], [pk[b]])
                    self.copy(Btok[:, c0:c0 + 4, g, :], PBh[b][:, 0:512].split(0, 128), [pk[b]])

            def make_cbm(g):
                for c0 in (0, 4):
                    b2 = bank((4, 5)) if False else bank((0, 1))
                    for j in range(4):
                        c = c0 + j
                        self.mm1(PB[b2][:, j * 128:(j + 1) * 128], BT[:, g, c * 128:(c + 1) * 128], CT[:, g, c * 128:(c + 1) * 128],
                                 True, True, [], [pk[b2]])
                    self.tt(CBm[:, 0, c0 * 128:(c0 + 4) * 128].split(0, 128), PB[b2].split(0, 128), Ub.bc(0, 4), ALU.mult, [pk[b2], k_const])
                    self.tt(CBm[:, 1, c0 * 128:(c0 + 4) * 128].split(0, 128), PB[b2].split(0, 128), Lb.bc(0, 4), ALU.mult, [pk[b2], k_const])
            m_tile = A.mark()
            zs = A.alloc([T], BF16)
            xc = A.alloc([T], BF16)
            xdt = A.alloc([2, 8, 128], BF16)
            xdw = A.alloc([8, 128], BF16)
            xfn = A.alloc([8, 128], BF16)
            hE = A.alloc([2, 8, 128], BF16)
            hm = [A.alloc([2, 128], F32) for _ in range(2)]
            h0s = A.alloc([2, 128], F32)
            dsegs = [A.alloc([512], F32) for _ in range(2)]
            segs = [A.alloc([512], BF16) for _ in range(2)]
            MC = [A.alloc([4, 512], BF16) for _ in range(2)]
            Mts = [[MC[hh_][:, 0], MC[hh_][:, 1]] for hh_ in range(2)]
            fss = [A.alloc([512], BF16) for _ in range(2)]
            Cfs = [[MC[hh_][:, 2], MC[hh_][:, 3]] for hh_ in range(2)]
            dseg, seg, fs, Mt = dsegs[0], segs[0], fss[0], Mts[0]
            tq_alias = V(A.t32, MC[1].off // 2, A.nbytes // 4, 0, 128, [(1, 512)], MC[1].aid)
            tq = tq_alias
            yg = A.alloc([512], F32)
            sqy = MC[1][:, 2]
            ygb = [MC[1][:, 3], MC[1][:, 3]]
            stf = [yg, dsegs[1]]
            nyg = 0
            nprep = {"n": 0}
            SQB = (4, 5)
            for j in range(16):
                g = j // 8
                if j % 8 == 0:
                    make_cbm(g)

                def z_evac(ti, h, pv, bkey):
                    self.act(zs[:, HS(h)], pv, AF.Silu, [bkey])

                def x_evac(ti, h, pv, bkey):
                    self.copy(uraw[:, 1 + h * 512:1 + (h + 1) * 512], pv, [bkey], eng="act")
                if j == 0:
                    linear(w_in_d[l], 0, D, O_XBC, [0], hsrc, x_evac, banks=(0, 1))
                conv_tile(uraw, acc, j, xc, None)
                linear(w_in_d[l], 0, D, O_Z + j * 128, [0], hsrc, z_evac, banks=(0, 1))
                xbk = {}
                for c0 in (0, 4):
                    b = bank((6, 7))
                    xbk[c0] = b
                    for jj in range(4):
                        c = c0 + jj
                        self.transpose(PBh[b][:, jj * 128:(jj + 1) * 128], xc[:, c * 128:(c + 1) * 128], identb, [k_const], [pk[b]])
                stb = {}
                for d_ in range(2):
                    for c0 in (0, 4):
                        b = xbk[c0]
                        xps = PBh[b][:, 0:512].split(0, 128).split(1, 64)
                        for dst, S in ((xdt[:, d_], S1), (xdw, S2), (xfn, S3)):
                            sc = S[:, c0:c0 + 4, d_ * 32 + 2 * j:d_ * 32 + 2 * j + 2].bc(2, 64)
                            self.tt(dst[:, c0:c0 + 4, :].split(1, 64), xps, sc, ALU.mult, [pk[b]])
                    for c0 in (0, 4):
                        b = 2 * d_ + c0 // 4
                        stb[(d_, c0)] = b
                        for jj in range(4):
                            c = c0 + jj
                            self.mm1(PB[b][:, jj * 128:(jj + 1) * 128], Btok[:, c, g, :], xdw[:, c, :], True, True, [], [pk[b]])
                    b = bank((4, 5))
                    for sq_ in range(4):
                        self.mm(PB[b][:, sq_ * 128:(sq_ + 1) * 128],
                                [(xfn[:, 2 * sq_, :], Btok[:, 2 * sq_, g, :]), (xfn[:, 2 * sq_ + 1, :], Btok[:, 2 * sq_ + 1, g, :])],
                                [], [pk[b]])
                    self.copy(stf[d_], PB[b], [pk[b]], eng="act")
                    self.dma(st_o[l, :, d_, j * 128:(j + 1) * 128, :].rearrange("s p n -> p s n"), stf[d_].split(0, 128), [],
                             [("sto", l, d_, j)])
                for d_ in range(2):
                    self.dma(h0s[:, d_, :], h0T_d[l, d_, :, j * 128:(j + 1) * 128])
                for d_ in range(2):
                    order = list(range(8)) if d_ == 0 else list(range(7, -1, -1))
                    self.copy(hE[:, d_, order[0], :], h0s[:, d_, :], eng="act")
                    prev = h0s[:, d_, :]
                    for n_ in range(7):
                        c = order[n_]
                        cn = order[n_ + 1]
                        b = stb[(d_, (c // 4) * 4)]
                        dcv = decbc[:, c, d_ * 32 + 2 * j:d_ * 32 + 2 * j + 2].bc(1, 64)
                        cur = hm[n_ % 2][:, d_, :]
                        self.tt(cur.split(0, 64), prev.split(0, 64), dcv, ALU.mult)
                        self.tt(cur, cur, PB[b][:, (c % 4) * 128:(c % 4 + 1) * 128], ALU.add, [pk[b]])
                        self.copy(hE[:, d_, cn, :], cur, eng="act")
                        prev = cur
                if j + 1 < 16:
                    linear(w_in_d[l], 0, D, O_XBC + (j + 1) * 128, [0], hsrc, x_evac, banks=(0, 1))
                blocks = [(h, hh) for h in (0, 1) for hh in (0, 1)]
                ybank = {0: bank((2, 3)), 1: None}
                ybank[1] = 5 - ybank[0]
                selb = {}

                def emit_sel(bi):
                    h, hh = blocks[bi]
                    for d_ in range(2):
                        row = d_ * 32 + 2 * j + hh
                        rb = bank((6, 7))
                        selb[(bi, d_)] = rb
                        sel = V(identf.t, identf.off + row, identf.pstride, 0, 64, [(0, 128)], identf.aid)
                        self.mm1(PB[rb], sel, cumT[R64, HS(h)], True, True, [k_const], [pk[rb]])

                def it_bufs(it):
                    return dsegs[it % 2], segs[it % 2], fss[it % 2]

                def stage1(it):
                    bi, d_ = it // 2, it % 2
                    h, hh = blocks[bi]
                    row = d_ * 32 + 2 * j + hh
                    dseg_, seg_, fs_ = it_bufs(it)
                    rb = selb[(bi, d_)]
                    self.tt(dseg_.split(0, 128), PB[rb].split(0, 128), cumtok[:, 4 * h:4 * h + 4, row].bc(1, 128),
                            ALU.subtract, [pk[rb]])
                    self.ts(dseg_, dseg_, 0.0, ALU.min)
                    self.act(seg_, dseg_, AF.Exp)
                    self.act(fs_, PB[rb], AF.Exp, [pk[rb], ("b", dseg_.aid)])

                def stage2(it):
                    bi, d_ = it // 2, it % 2
                    h, hh = blocks[bi]
                    dseg_, seg_, fs_ = it_bufs(it)
                    self.tt(Mts[hh][d_], seg_, CBm[:, d_, HS(h)], ALU.mult)
                    self.tt(Cfs[hh][d_], CT[:, g, HS(h)], fs_, ALU.mult)

                emit_sel(0)
                stage1(0)
                for it in range(8):
                    bi, d_ = it // 2, it % 2
                    h, hh = blocks[bi]
                    yb = ybank[h]
                    if d_ == 1 and bi + 1 < len(blocks):
                        emit_sel(bi + 1)
                    if it + 1 < 8:
                        stage1(it + 1)
                    stage2(it)
                    if d_ == 0:
                        continue
                    for cc in range(4):
                        c = 4 * h + cc
                        out = PB[yb][hh * 64:(hh + 1) * 64, cc * 128:(cc + 1) * 128]
                        pairs = []
                        for dd in range(2):
                            pairs.append((xdt[:, dd, c, hh * 64:(hh + 1) * 64], Mts[hh][dd][:, cc * 128:(cc + 1) * 128]))
                            pairs.append((hE[:, dd, c, hh * 64:(hh + 1) * 64], Cfs[hh][dd][:, cc * 128:(cc + 1) * 128]))
                        self.mm(out, pairs, [], [pk[yb]])
                    if hh == 0:
                        continue
                    self.stt(tq, xc[:, HS(h)], dcol[:, j:j + 1], PB[yb], ALU.mult, ALU.add, [pk[yb], LK])
                    self.tt(yg, tq, zs[:, HS(h)], ALU.mult)
                    self.act(sqy, yg, AF.Square)
                    sb_ = bank((4, 5))
                    self.mm1(PB[sb_], onesb, sqy, True, True, [k_const], [pk[sb_]])
                    if j == 0:
                        self.copy(rstdB[:, HS(h)], PB[sb_], [pk[sb_]], [("ssq", h)])
                    else:
                        self.tt(rstdB[:, HS(h)], rstdB[:, HS(h)], PB[sb_], ALU.add, [pk[sb_], ("ssq", h)], [("ssq", h)])
                    y_b = ygb[nyg % 2]
                    nyg += 1
                    self.act(y_b, yg, AF.Identity, [LK], scale=sng[:, j:j + 1])
                    self.dma(ygs_s[j * 128:(j + 1) * 128, HS(h)], y_b, [], [("ygs", j, h)])
            A.release(m_shared)
            lnv = lnvB
            for h in (0, 1):
                self.act(lnv, rstdB[:, HS(h)], AF.Ln, [("ssq", h)], bias=epsc, scale=1.0 / D)
                self.act(rstdB[:, HS(h)], lnv, AF.Exp, [], [("rstdB", h)], scale=-0.5)
            ygT = A.alloc([16, T], BF16)
            for j in range(16):
                for h in (0, 1):
                    self.dma(ygT[:, j, HS(h)], ygs_s[j * 128:(j + 1) * 128, HS(h)], [("ygs", j, h)], [("ygT", j, h)])
            if "yb" in dbg_o and l == 0:
                dbg_dump("yb", ygT, [("ygT", j, h) for j in range(16) for h in (0, 1)], 128)
            merge_branch(l, O_GB, w_ob_d, D, lambda k_, h: (ygT[:, k_, HS(h)], ("ygT", k_, h)), 1, rstdB)
            A.release(m_phase)

        for l in range(NL):
            LK = ("lp", l)
            self.P.tag = f"L{l}.mod"
            for v_, d_ in ((n1g, n1g_d[l]), (n2g, n2g_d[l]), (dcol, dcol_d[l]), (sng, sng_d[l]),
                           (sinkb, sink_d[l]), (convb, convb_d[l]), (cqn, cqn_d[l]), (ckn, ckn_d[l]),
                           (alogc[0:64], alog_d[l]), (dtbc[0:64], dtb_d[l])):
                self.dma(v_, d_, (), [LK])
            self.dma(convw, convw_d[l], (), [LK])
            modT, badaT_l = modTs[l % 2], badaTs[l % 2]
            if l == 0:
                m_mod0 = A.mark()
                self.dma(badaTs[0], badaT_d[0], (), [("bada", 0)])
                ex0 = add_slots()
                for _ in mod_gen(0, range(0, 8)):
                    pass
                drop_slots(ex0)
                A.release(m_mod0)
                bg.append(mod_gen(0, range(8, 24)))
            else:
                bg_drain()
            MODK = [("modT", l, p) for p in range(24)]
            self.stt(s1c, modT[:, 16:32], 1.0, n1g, ALU.add, ALU.mult, [("modT", l, p) for p in range(4, 8)] + [LK], ["cols"])
            sh1, g1c, sh2, g2c = modT[:, 0:16], modT[:, 32:48], modT[:, 48:64], modT[:, 80:96]
            MK = "cols"
            self.P.tag = f"L{l}.norm1"
            norm_mod(s1c, sh1, [MK] + [("modT", l, p) for p in range(0, 4)])
            ph = self.phases
            if ph is None or "a" in ph:
                self.P.tag = f"L{l}.attnA"
                attn_phase(l, "a")
            if l + 1 < NL:
                self.dma(badaTs[(l + 1) % 2], badaT_d[l + 1], (), [("bada", l + 1)])
                bg.append(mod_gen(l + 1, range(0, 24)))
            if ph is None or "c" in ph:
                self.P.tag = f"L{l}.attnC"
                attn_phase(l, "c")
            if ph is None or "b" in ph:
                self.P.tag = f"L{l}.ssd"
                ssd_phase(l)
            self.P.tag = f"L{l}.out"
            if ph is not None and "out" not in ph:
                continue
            bg_drain()
            self.stt(s2c, modT[:, 64:80], 1.0, n2g, ALU.add, ALU.mult, [("modT", l, p) for p in range(16, 20)] + [LK], ["cols2"])
            m = A.mark()
            mrgT = A.alloc([16, T], BF16)
            mtmp = [A.alloc([512], BF16) for _ in range(12)]
            ex_slots = add_slots()
            nn = 0
            for k_ in range(16):
                for h in (0, 1):
                    self.dma(mrgT[:, k_, HS(h)], mrg_s[0, k_ * 128:(k_ + 1) * 128, HS(h)], [("mrg", 0, k_, h)], [("mrgT", k_, h)])
                    for bi in (1, 2):
                        t_ = mtmp[nn % 12]
                        nn += 1
                        self.dma(t_, mrg_s[bi, k_ * 128:(k_ + 1) * 128, HS(h)], [("mrg", bi, k_, h)])
                        self.tt(mrgT[:, k_, HS(h)], mrgT[:, k_, HS(h)], t_, ALU.add, [("mrgT", k_, h)], [("mrgT", k_, h)])

            def out_evac(ti, h, pv, bkey):
                self.stt(xT[:, ti, HS(h)], pv, g1c[:, ti:ti + 1], xT[:, ti, HS(h)], ALU.mult, ALU.add, [bkey, "xT"] + [("modT", l, p) for p in range(8, 12)], ["xT"])
            linear(w_out_d[l], 0, D, 0, [i * 128 for i in range(16)], lambda k_, h: (mrgT[:, k_, HS(h)], ("mrgT", k_, h)), out_evac)
            drop_slots(ex_slots)
            A.release(m)
            if "x1" in dbg_o and l == 0:
                dbg_dump("x1", xT, ["xT"], 128)
            self.P.tag = f"L{l}.mlp"
            norm_mod(s2c, sh2, ["cols2"] + [("modT", l, p) for p in range(12, 16)])
            m = A.mark()
            f1 = A.alloc([16, T], BF16)
            rl = [A.alloc([512], F32) for _ in range(2)]
            ex_slots = add_slots()
            nr = {"n": 0}
            for fb in range(4):
                def f1_evac(ti, h, pv, bkey):
                    r_ = rl[nr["n"] % 2]
                    nr["n"] += 1
                    self.act(r_, pv, AF.Relu, [bkey])
                    self.tt(f1[:, ti, HS(h)], pv, r_, ALU.mult, [bkey], [("f1", ti, h)])

                def f2_evac(ti, h, pv, bkey):
                    self.stt(xT[:, ti, HS(h)], pv, g2c[:, ti:ti + 1], xT[:, ti, HS(h)], ALU.mult, ALU.add, [bkey, "xT"] + [("modT", l, p) for p in range(20, 24)], ["xT"])
                linear(w_m1_d[l], 0, D, fb * 2048, [i * 128 for i in range(16)], hsrc, f1_evac)
                linear(w_m2_d[l], fb * 2048, 2048, 0, [i * 128 for i in range(16)],
                       lambda k_, h: (f1[:, k_, HS(h)], ("f1", k_, h)), f2_evac)
            drop_slots(ex_slots)
            A.release(m)

        self.P.tag = "final"
        m = A.mark()
        rstd = A.alloc([T], F32)
        norm_stats(lambda i, h: (xT[:, i, HS(h)], "xT"), 16, D, rstd, "rstdF")
        stg = [A.alloc([512], F32) for _ in range(2)]
        n = 0
        for h in (0, 1):
            for c in range(16):
                s_ = stg[n % 2]
                n += 1
                self.stt(s_, xT[:, c, HS(h)], fng[:, c:c + 1], rstd[:, HS(h)], ALU.mult, ALU.mult,
                         ["xT", ("rstdF", h), k_const])
                self.dma(yT_o[c * 128:(c + 1) * 128, HS(h)], s_, [], [("yT", c, h)])
        A.release(m)
        self.P.emit(nc)
        return nc


def _rope_tables():
    rows = T // 64
    row = np.repeat(np.arange(rows, dtype=np.float32), 64)
    col = np.tile(np.arange(64, dtype=np.float32), rows)
    n_freq = 32
    inv = (10000.0 ** (-np.arange(n_freq, dtype=np.float32) / n_freq)).astype(np.float32)
    ang = np.concatenate([row[:, None] * inv, col[:, None] * inv], axis=-1)
    return np.cos(ang).astype(np.float32), np.sin(ang).astype(np.float32)


def make_in_maps(inp):
    f = np.float32
    cos, sin = _rope_tables()
    cosT_s = np.ascontiguousarray(np.concatenate([cos, cos], axis=1).T)
    sinT_s = np.ascontiguousarray(np.concatenate([-sin, sin], axis=1).T)
    ident = np.eye(128, dtype=f)
    rotm = np.zeros((128, 128), f)
    for dp in range(128):
        rotm[(dp + 64) % 128, dp] = 1.0
    U = np.triu(np.ones((128, 128), f))
    L = np.tril(np.ones((128, 128), f))

    def colT(v):
        return np.ascontiguousarray(np.asarray(v, f).reshape(-1, 128).T)

    shared = {
        "n1g": np.stack([colT(inp["norm1_g"][l]) for l in range(DEPTH)]),
        "n2g": np.stack([colT(inp["norm2_g"][l]) for l in range(DEPTH)]),
        "fng": colT(inp["final_norm_g"]),
        "badaT": np.stack([colT(inp["b_ada"][l]) for l in range(DEPTH)]),
        "sinkbc": np.ascontiguousarray(np.broadcast_to(np.asarray(inp["a_sink"], f)[:, None, :], (DEPTH, 128, 8))),
        "convw": np.stack([np.ascontiguousarray(np.asarray(inp["conv_w"][l], f).T.reshape(20, 128, 3).transpose(1, 0, 2))
                           for l in range(DEPTH)]),
        "convb": np.stack([colT(inp["conv_b"][l]) for l in range(DEPTH)]),
        "alog": np.asarray(inp["ssm_a_log"], f).reshape(DEPTH, 64, 1),
        "dtb": np.asarray(inp["ssm_dt_bias"], f).reshape(DEPTH, 64, 1),
        "dcol": np.stack([colT(np.repeat(np.asarray(inp["ssm_d"][l], f), 64)) for l in range(DEPTH)]),
        "sng": np.stack([colT(inp["ssm_norm_g"][l]) for l in range(DEPTH)]),
        "cqn": np.asarray(inp["c_q_norm"], f).reshape(DEPTH, 128, 1),
        "ckn": np.asarray(inp["c_k_norm"], f).reshape(DEPTH, 128, 1),
        "ident": ident, "rot": rotm, "U": U, "L": L,
    }
    for k_ in ("w_ada", "w_in", "w_oa", "w_ob", "w_oc", "w_out", "w_mlp1", "w_mlp2"):
        shared[k_] = np.asarray(inp[k_], f)
    tri_prev = np.triu(np.ones((128, 128), f))
    tri_prev = np.ascontiguousarray(tri_prev.T)
    tri_next = np.ascontiguousarray(np.triu(np.ones((128, 128), f)))
    maps = []
    for core in range(8):
        m = dict(shared)
        if core < 4:
            b = core
            m["xT"] = np.ascontiguousarray(np.asarray(inp["x_sample"][b], f).T)
            m["cond"] = colT(inp["c"][b])
            m["cosT"], m["sinT"] = cosT_s, sinT_s
            m["akT_ctx"] = np.ascontiguousarray(np.asarray(inp["cache_a_k"][b], f).transpose(0, 2, 3, 1))
            m["av_ctx"] = np.ascontiguousarray(np.asarray(inp["cache_a_v"][b], f).transpose(0, 2, 1, 3))
            m["ckT_ctx"] = np.ascontiguousarray(np.asarray(inp["cache_c_k"][b], f).transpose(0, 2, 3, 1))
            m["cv_ctx"] = np.ascontiguousarray(np.asarray(inp["cache_c_v"][b], f).transpose(0, 2, 1, 3))
            m["h0T"] = np.ascontiguousarray(np.asarray(inp["state_ssm"][b], f).reshape(DEPTH, 2, 2048, 128).transpose(0, 1, 3, 2))
            amprev = np.broadcast_to(tri_prev[:, None, :], (128, 8, 128)).copy()
            amnext = np.broadcast_to(tri_next[:, None, :], (128, 8, 128)).copy()
            m["actxb"] = np.zeros((128, 1), f)
            m["cbias"] = np.zeros((128, 80), f)
            m["kmul"] = np.ones((128, 8, 2), f)
            m["negflag"] = np.zeros((128, 1), f)
        else:
            j = core - 4
            xp = np.asarray(inp["x_prompt"][4 * j:4 * j + 4], f).reshape(T, D)
            m["xT"] = np.ascontiguousarray(xp.T)
            m["cond"] = colT(inp["c_ctx"])
            m["cosT"], m["sinT"] = np.ones((128, T), f), np.zeros((128, T), f)
            for k_ in ("akT_ctx", "ckT_ctx"):
                m[k_] = np.zeros((DEPTH, 2, 128, 256), f)
            for k_ in ("av_ctx", "cv_ctx"):
                m[k_] = np.zeros((DEPTH, 2, 256, 128), f)
            m["h0T"] = np.zeros((DEPTH, 2, 128, 2048), f)
            amprev = np.zeros((128, 8, 128), f)
            amnext = np.zeros((128, 8, 128), f)
            amprev[:, 1::2, :] = 1.0
            amnext[:, 0::2, :] = 1.0
            m["actxb"] = np.full((128, 1), NEG, f)
            cb = np.full((8, 10), NEG, f)
            for i in range(8):
                s = i // 2
                cb[i, 2 + 2 * s] = 0.0
                cb[i, 2 + 2 * s + 1] = 0.0
            m["cbias"] = np.ascontiguousarray(np.broadcast_to(cb.reshape(1, 80), (128, 80)))
            km = np.zeros((8, 2), f)
            km[0::2, 0] = 1.0
            km[1::2, 1] = 1.0
            m["kmul"] = np.ascontiguousarray(np.broadcast_to(km[None], (128, 8, 2)))
            m["negflag"] = np.full((128, 1), -1.0, f)
        m["amprev"], m["amnext"] = amprev, amnext
        maps.append(m)
    return maps


_CACHE = {}


def kernel(**inputs):
    if "k" not in _CACHE:
        kb = K()
        kb.build()
        _CACHE["k"] = kb
    kb = _CACHE["k"]
    maps = make_in_maps(inputs)
    maps = [{k_: v for k_, v in m.items() if k_ in kb.inputs} for m in maps]
    res = run_bass_kernel_spmd(kb.nc, maps, core_ids=list(range(8)))
    R = res.results
    f = np.float32
    y_sample = np.stack([R[b]["yT"].T for b in range(4)]).astype(f)
    y_prompt = np.concatenate([R[4 + j]["yT"].T.reshape(4, 256, D) for j in range(4)]).astype(f)

    def cache(name):
        outs = []
        for j in range(4):
            a = R[4 + j][name]
            a = a.reshape(DEPTH, 2, 128, 4, 256)
            outs.append(a.transpose(3, 0, 4, 1, 2))
        return np.ascontiguousarray(np.concatenate(outs)).astype(f)

    st = np.concatenate([R[4 + j]["st_o"].transpose(1, 0, 2, 3, 4) for j in range(4)])
    st = np.ascontiguousarray(st.reshape(16, DEPTH, 2, 32, 64, 128)).astype(f)
    return (y_prompt, y_sample, cache("akT_o"), cache("avT_o"), cache("ckT_o"), cache("cvT_o"), st)
```
